# Optimizing a Trainium2 kernel written in Bass

```python
import math
import jax
import jax.numpy as jnp
from jax import lax
import numpy as np

D_MODEL = 1024
BATCH = 2
SEQ = 8192
DEPTH = 2

HEAD_DIM = 64
SB_HEADS = 8
SB_WIDTH = SB_HEADS * HEAD_DIM
CONV_GROUPS = 8
CONV_WIDTH = D_MODEL - SB_WIDTH
CONV_GROUP_DIM = CONV_WIDTH // CONV_GROUPS
CONV_K = 3
EVEN_IN = 3 * SB_WIDTH + 3 * CONV_WIDTH
DIFF_HEADS = 8
DIFF_QK_DIM = 64
DIFF_V_DIM = 2 * DIFF_QK_DIM
DIFF_WIDTH = DIFF_HEADS * DIFF_V_DIM
ODD_IN = DIFF_HEADS * (4 * DIFF_QK_DIM + DIFF_V_DIM)
D_FF = 4 * D_MODEL
N_EVEN = (DEPTH + 1) // 2
N_ODD = DEPTH // 2
Q_BLOCK = 128
NORM_EPS = 1e-6

kernel_name = 'hybrid_stickbreak_shortconv_diffattn'


def rmsnorm(x, g):
    xf = x.astype(jnp.float32)
    y = xf * lax.rsqrt(jnp.mean(jnp.square(xf), axis=-1, keepdims=True) + NORM_EPS)
    return (y * g.astype(jnp.float32)).astype(x.dtype)


def split_heads(t, n_heads):
    b, s, _ = t.shape
    return t.reshape(b, s, n_heads, -1).transpose(0, 2, 1, 3)


def merge_heads(t):
    b, h, s, d = t.shape
    return t.transpose(0, 2, 1, 3).reshape(b, s, h * d)


def stick_breaking_attention(q, k, v):
    seq = q.shape[2]
    scale = HEAD_DIM ** -0.5
    outs = []
    for start in range(0, seq, Q_BLOCK):
        end = start + Q_BLOCK
        z = jnp.einsum('bhqd,bhkd->bhqk', q[:, :, start:end], k[:, :, :end]).astype(jnp.float32) * scale
        past = jnp.arange(end)[None, :] < jnp.arange(start, end)[:, None]
        log_beta = jax.nn.log_sigmoid(z)
        log_keep = jnp.where(past, log_beta - z, 0.0)
        later = lax.cumsum(log_keep, axis=3, reverse=True) - log_keep
        w = jnp.where(past, jnp.exp(log_beta + later), 0.0)
        outs.append(jnp.einsum('bhqk,bhkd->bhqd', w.astype(v.dtype), v[:, :, :end]))
    return jnp.concatenate(outs, axis=2)


def short_gated_conv(b_gate, c_gate, u, w):
    seq = u.shape[1]
    cu = c_gate * u
    padded = jnp.pad(cu, ((0, 0), (CONV_K - 1, 0), (0, 0)))
    y = sum(padded[:, j:j + seq] * w[j] for j in range(CONV_K))
    return b_gate * y


def differential_attention(q1, q2, k1, k2, v, lam):
    seq = q1.shape[2]
    scale = DIFF_QK_DIM ** -0.5
    lam = lam.astype(jnp.float32)
    outs = []
    for start in range(0, seq, Q_BLOCK):
        end = start + Q_BLOCK
        causal = jnp.arange(end)[None, :] <= jnp.arange(start, end)[:, None]
        s1 = jnp.einsum('bhqd,bhkd->bhqk', q1[:, :, start:end], k1[:, :, :end]).astype(jnp.float32) * scale
        s2 = jnp.einsum('bhqd,bhkd->bhqk', q2[:, :, start:end], k2[:, :, :end]).astype(jnp.float32) * scale
        p1 = jax.nn.softmax(jnp.where(causal, s1, -jnp.inf), axis=-1)
        p2 = jax.nn.softmax(jnp.where(causal, s2, -jnp.inf), axis=-1)
        w = p1 - lam * p2
        outs.append(jnp.einsum('bhqk,bhkd->bhqd', w.astype(v.dtype), v[:, :, :end]))
    return jnp.concatenate(outs, axis=2)


def setup_inputs(seed: int = 0) -> dict:
    key = jax.random.key(seed)
    ks = jax.random.split(key, 16)

    def normal(k, shape, scale):
        return jax.random.normal(k, shape, jnp.float32) * scale

    return {
        'x': normal(ks[0], (BATCH, SEQ, D_MODEL), 1.0),
        'norm_mix': 1.0 + normal(ks[1], (DEPTH, D_MODEL), 0.02),
        'norm_mlp': 1.0 + normal(ks[2], (DEPTH, D_MODEL), 0.02),
        'norm_final': 1.0 + normal(ks[3], (D_MODEL,), 0.02),
        'w_in_even': normal(ks[4], (N_EVEN, D_MODEL, EVEN_IN), D_MODEL ** -0.5),
        'conv_w': normal(ks[5], (N_EVEN, CONV_K, CONV_WIDTH), CONV_K ** -0.5),
        'w_out_even': normal(ks[6], (N_EVEN, SB_WIDTH + CONV_WIDTH, D_MODEL), (SB_WIDTH + CONV_WIDTH) ** -0.5),
        'w_in_odd': normal(ks[7], (N_ODD, D_MODEL, ODD_IN), D_MODEL ** -0.5),
        'lam_q1': normal(ks[8], (N_ODD, DIFF_QK_DIM), 0.1),
        'lam_k1': normal(ks[9], (N_ODD, DIFF_QK_DIM), 0.1),
        'lam_q2': normal(ks[10], (N_ODD, DIFF_QK_DIM), 0.1),
        'lam_k2': normal(ks[11], (N_ODD, DIFF_QK_DIM), 0.1),
        'subln_g': 1.0 + normal(ks[12], (N_ODD, DIFF_V_DIM), 0.02),
        'w_out_odd': normal(ks[13], (N_ODD, DIFF_WIDTH, D_MODEL), DIFF_WIDTH ** -0.5),
        'w_up': normal(ks[14], (DEPTH, D_MODEL, D_FF), D_MODEL ** -0.5),
        'w_down': normal(ks[15], (DEPTH, D_FF, D_MODEL), D_FF ** -0.5),
    }


def reference(x, norm_mix, norm_mlp, norm_final, w_in_even, conv_w, w_out_even,
              w_in_odd, lam_q1, lam_k1, lam_q2, lam_k2, subln_g, w_out_odd, w_up, w_down):
    b, s, _ = x.shape
    for layer in range(DEPTH):
        i = layer // 2
        h = rmsnorm(x, norm_mix[layer])
        if layer % 2 == 0:
            proj = h @ w_in_even[i]
            q, k, v, b_gate, c_gate, u = jnp.split(
                proj,
                [SB_WIDTH, 2 * SB_WIDTH, 3 * SB_WIDTH,
                 3 * SB_WIDTH + CONV_WIDTH, 3 * SB_WIDTH + 2 * CONV_WIDTH],
                axis=-1)
            a_out = merge_heads(stick_breaking_attention(
                split_heads(q, SB_HEADS), split_heads(k, SB_HEADS), split_heads(v, SB_HEADS)))
            c_out = short_gated_conv(b_gate, c_gate, u, conv_w[i])
            mix = jnp.concatenate([a_out, c_out], axis=-1) @ w_out_even[i]
        else:
            proj = h @ w_in_odd[i]
            qk_w = DIFF_HEADS * 2 * DIFF_QK_DIM
            q, k, v = jnp.split(proj, [qk_w, 2 * qk_w], axis=-1)
            q = q.reshape(b, s, DIFF_HEADS, 2, DIFF_QK_DIM).transpose(0, 2, 3, 1, 4)
            k = k.reshape(b, s, DIFF_HEADS, 2, DIFF_QK_DIM).transpose(0, 2, 3, 1, 4)
            lambda_init = 0.8 - 0.6 * math.exp(-0.3 * layer)
            lam = (jnp.exp(jnp.sum(lam_q1[i] * lam_k1[i]))
                   - jnp.exp(jnp.sum(lam_q2[i] * lam_k2[i])) + lambda_init)
            o = differential_attention(q[:, :, 0], q[:, :, 1], k[:, :, 0], k[:, :, 1],
                                       split_heads(v, DIFF_HEADS), lam)
            o = rmsnorm(o, subln_g[i]) * (1.0 - lambda_init)
            mix = merge_heads(o) @ w_out_odd[i]
        x = x + mix
        h = rmsnorm(x, norm_mlp[layer])
        x = x + jnp.square(jax.nn.relu(h @ w_up[layer])) @ w_down[layer]
    return rmsnorm(x, norm_final)
```

```python
import math
from contextlib import ExitStack

import numpy as np
import ml_dtypes

import concourse.bass as bass
import concourse.mybir as mybir
from concourse.bass_utils import run_bass_kernel_spmd

F32 = mybir.dt.float32
BF16 = mybir.dt.bfloat16
AF = mybir.ActivationFunctionType
ALU = mybir.AluOpType
NPBF = ml_dtypes.bfloat16

DEBUG = {}
D = 1024
DFF = 4096
EPS = 1e-6


class T:
    def __init__(self, t, name=""):
        self.t = t
        self.name = name
        self.w = {}
        self.r = {}
        self.dsem = None
        self.dcount = 0
        self.psum = False

    def __getitem__(self, idx):
        return self.t[idx]


class _Rec:
    def __init__(self):
        self.calls = []

    def __getattr__(self, name):
        def f(*a, **k):
            self.calls.append((name, a, k))
        return f


class Prog:
    ENGS = ("tensor", "vector", "scalar", "gpsimd", "sync")

    def __init__(self, nc, stack):
        self.nc = nc
        self.stack = stack
        self.ops = {e: [] for e in self.ENGS}
        self.sem = {}
        self.count = {e: 0 for e in self.ENGS}
        self.pending = {e: False for e in self.ENGS}
        self.seen = {e: {} for e in self.ENGS}
        for e in ("tensor", "vector", "scalar", "gpsimd"):
            self.sem[e] = stack.enter_context(nc.semaphore("s_" + e))
        self.ninst = 0
        self.nwaits = 0
        self.uid = 0
        self.semstack = stack
        self.prefix = ""
        self.dma_toks = {}
        self._consts = None

    def sb(self, name, shape, dt, stack=None):
        st = stack or self.stack
        name = self.prefix + name
        return T(st.enter_context(self.nc.sbuf_tensor(name, shape, dt)), name)

    def ps(self, name, shape, dt, stack=None):
        st = stack or self.stack
        name = self.prefix + name
        t = T(st.enter_context(self.nc.psum_tensor(name, shape, dt)), name)
        t.psum = True
        return t

    def dram(self, t, name=""):
        return T(t, name)

    def _dsem(self, t):
        if t.dsem is None:
            self.uid += 1
            t.dsem = self.semstack.enter_context(self.nc.semaphore("d%d_%s" % (self.uid, t.name)))
        return t.dsem

    def _deps(self, eng, reads, writes):
        need = {}
        own = self.sem.get(eng)
        for t in reads:
            for s, v in t.w.items():
                if need.get(s, 0) < v:
                    need[s] = v
            if t.psum:
                for s, v in t.r.items():
                    if s is not own and need.get(s, 0) < v:
                        need[s] = v
        for t in writes:
            for d in (t.w, t.r):
                for s, v in d.items():
                    if need.get(s, 0) < v:
                        need[s] = v
        waits = []
        seen = self.seen[eng]
        for s, v in need.items():
            if eng == "tensor" and s is self.sem["tensor"]:
                continue
            if seen.get(s, 0) < v:
                seen[s] = v
                waits.append((s, v))
        return waits

    def _post(self, tok, reads, writes):
        s, v = tok
        for t in reads:
            if t.r.get(s, 0) < v:
                t.r[s] = v
        for t in writes:
            t.w = {s: v}
            t.r = {}

    def op(self, eng, fn, reads=(), writes=(), signal=True):
        waits = self._deps(eng, reads, writes)
        sem = self.sem[eng]
        if signal:
            self.count[eng] += 1
            val = self.count[eng]
            self.pending[eng] = False
        else:
            val = self.count[eng] + 1
            self.pending[eng] = True
        self.ninst += 1
        self.nwaits += len(waits)
        rec = _Rec()
        fn(rec)
        (name, a, k), = rec.calls

        def run(e, waits=waits, name=name, a=a, k=k, signal=signal, sem=sem):
            for s, v in waits:
                e.wait_ge(s, v)
            ins = getattr(e, name)(*a, **k)
            if signal:
                ins.then_inc(sem, 1)

        self.ops[eng].append(run)
        self._post((sem, val), reads, writes)

    def dma(self, eng, out_t, out_ap, in_t, in_ap, disjoint=False, owner=None):
        in_ts = list(in_t) if isinstance(in_t, (list, tuple)) else [in_t]
        waits = self._deps(eng, in_ts, [] if disjoint else [out_t])
        so = owner if owner is not None else out_t
        sem = self._dsem(so)
        so.dcount += 16
        val = so.dcount
        self.dma_toks[sem] = val
        self.ninst += 1
        self.nwaits += len(waits)

        def run(e, waits=waits, sem=sem):
            for s, v in waits:
                e.wait_ge(s, v)
            e.dma_start(out=out_ap, in_=in_ap).then_inc(sem, 16)

        self.ops[eng].append(run)
        if disjoint:
            self._post((sem, val), in_ts, [])
            out_t.w[sem] = val
        else:
            self._post((sem, val), in_ts, [out_t])

    def group_done(self, owner, tiles):
        for t in tiles:
            t.w = {owner.dsem: owner.dcount}

    def cc_allgather(self, out_t, out_ap, in_t, in_ap, groups):
        waits = self._deps("gpsimd", [in_t], [out_t])
        sem = self._dsem(out_t)
        out_t.dcount += 1
        val = out_t.dcount
        self.ninst += 1

        def run(e, waits=waits, sem=sem):
            for s, v in waits:
                e.wait_ge(s, v)
            e.collective_compute("AllGather", ALU.bypass, replica_groups=groups,
                                 ins=[in_ap.opt()], outs=[out_ap.opt()]).then_inc(sem)

        self.ops["gpsimd"].append(run)
        self._post((sem, val), [in_t], [out_t])

    def drain(self, scratch):
        waits = []
        seen = self.seen["gpsimd"]
        for s, v in self.dma_toks.items():
            if seen.get(s, 0) < v:
                seen[s] = v
                waits.append((s, v))

        def run(e, waits=waits):
            for s, v in waits:
                e.wait_ge(s, v)

        self.ops["gpsimd"].append(run)
        self.op("gpsimd", lambda e: e.memset(scratch[:], 0.0), writes=[scratch])
        self.barrier()

    def wait_all(self, eng, tiles):
        waits = self._deps(eng, tiles, [])

        def run(e, waits=waits):
            for s, v in waits:
                e.wait_ge(s, v)

        self.ops[eng].append(run)

    def barrier(self):
        for e in self.ENGS:
            waits = []
            for o in ("tensor", "vector", "scalar", "gpsimd"):
                assert not self.pending[o], o
                v = self.count[o]
                s = self.sem[o]
                if o == e or v == 0:
                    continue
                if self.seen[e].get(s, 0) < v:
                    self.seen[e][s] = v
                    waits.append((s, v))

            def run(eng, waits=waits):
                for s, v in waits:
                    eng.wait_ge(s, v)

            self.ops[e].append(run)

    def finish(self):
        for e in self.ENGS:
            assert not self.pending[e], e
        ops = self.ops
        with self.nc.Block() as block:
            @block.tensor
            def _(e):
                for f in ops["tensor"]:
                    f(e)

            @block.vector
            def _(e):
                for f in ops["vector"]:
                    f(e)

            @block.scalar
            def _(e):
                for f in ops["scalar"]:
                    f(e)

            @block.gpsimd
            def _(e):
                for f in ops["gpsimd"]:
                    f(e)

            @block.sync
            def _(e):
                for f in ops["sync"]:
                    f(e)


def make_consts(P):
    if P._consts is not None:
        return P._consts
    c = {}
    P._consts = c
    c["ident"] = P.sb("ident", [128, 128], BF16)
    c["ones"] = P.sb("ones", [128, 128], BF16)
    c["tri"] = P.sb("tri", [128, 128], BF16)
    c["epsb"] = P.sb("epsb", [128, 1], F32)
    c["oneb"] = P.sb("oneb", [128, 1], F32)
    ident, ones, tri = c["ident"], c["ones"], c["tri"]
    P.op("vector", lambda e: e.memset(c["epsb"][:], EPS), writes=[c["epsb"]])
    P.op("vector", lambda e: e.memset(c["oneb"][:], 1.0), writes=[c["oneb"]])
    P.op("gpsimd", lambda e: e.memset(ident[:], 0.0), writes=[ident])
    P.op("gpsimd", lambda e: e.affine_select(out=ident[:], in_=ident[:], pattern=[[-1, 128]],
                                              compare_op=ALU.not_equal, fill=1.0, base=0,
                                              channel_multiplier=1), reads=[ident], writes=[ident])
    P.op("gpsimd", lambda e: e.memset(ones[:], 1.0), writes=[ones])
    P.op("gpsimd", lambda e: e.memset(tri[:], 1.0), writes=[tri])
    P.op("gpsimd", lambda e: e.affine_select(out=tri[:], in_=tri[:], pattern=[[-1, 128]],
                                              compare_op=ALU.is_ge, fill=0.0, base=0,
                                              channel_multiplier=1), reads=[tri], writes=[tri])
    return c


def make_masks(P, strict, name):
    ms = []
    for dj in range(4):
        m = P.sb("%s%d" % (name, dj), [128, 2, 512], BF16)
        P.op("gpsimd", lambda e, m=m: e.memset(m[:], 1.0), writes=[m])
        P.op("gpsimd", lambda e, m=m, dj=dj: e.affine_select(
            out=m[:], in_=m[:], pattern=[[0, 2], [1, 512]],
            compare_op=(ALU.is_gt if strict else ALU.is_ge), fill=0.0, base=-dj * 128,
            channel_multiplier=-1), reads=[m], writes=[m])
        ms.append(m)
    return ms


class NormTr:
    def __init__(self, P, c, gain_t, pT_list, tag, stack=None):
        self.P, self.c, self.g = P, c, gain_t
        self.pT = pT_list
        self.i = 0
        self.junk = P.sb(tag + "junk", [128, 1024], BF16, stack)
        self.ss = [P.sb(tag + "ss%d" % i, [128, 1], F32, stack) for i in range(2)]
        self.ln = [P.sb(tag + "ln%d" % i, [128, 1], F32, stack) for i in range(2)]
        self.rs = [P.sb(tag + "rs%d" % i, [128, 1], F32, stack) for i in range(2)]
        self.hn = [P.sb(tag + "hn%d" % i, [128, 1024], BF16, stack) for i in range(2)]

    def stats(self, xt, xap):
        P, c = self.P, self.c
        i = self.i
        ss, ln, rs = self.ss[i % 2], self.ln[i % 2], self.rs[i % 2]
        junk = self.junk
        P.op("scalar", lambda e: e.activation(out=junk[:], in_=xap, func=AF.Square, scale=1.0 / 32,
                                              accum_out=ss[:]), reads=[xt], writes=[junk, ss])
        P.op("scalar", lambda e: e.activation(out=ln[:], in_=ss[:], func=AF.Ln, bias=c["epsb"][:]),
             reads=[ss, c["epsb"]], writes=[ln])
        P.op("scalar", lambda e: e.activation(out=rs[:], in_=ln[:], func=AF.Exp, scale=-0.5),
             reads=[ln], writes=[rs])
        return rs

    def pre(self, xt, xap):
        P = self.P
        rs = self.stats(xt, xap)
        i = self.i
        self.i += 1
        hn = self.hn[i % 2]
        g = self.g
        P.op("vector", lambda e: e.scalar_tensor_tensor(out=hn[:], in0=xap, scalar=rs[:], in1=g[:],
                                                        op0=ALU.mult, op1=ALU.mult),
             reads=[xt, rs, g], writes=[hn])
        return (i, hn)

    def post(self, h, hT, col0, evac_eng="vector"):
        P, c = self.P, self.c
        i, hn = h
        pT = self.pT[i % len(self.pT)]
        for k in range(8):
            P.op("tensor", lambda e, k=k: e.transpose(out=pT[:, k, :], in_=hn[:, k * 128:(k + 1) * 128],
                                                      identity=c["ident"][:]),
                 reads=[hn, c["ident"]], writes=[pT], signal=(k == 7))
        if evac_eng == "vector":
            P.op("vector", lambda e: e.tensor_copy(out=hT[:, :, col0:col0 + 128], in_=pT[:]),
                 reads=[pT], writes=[hT])
        else:
            P.op("scalar", lambda e: e.copy(out=hT[:, :, col0:col0 + 128], in_=pT[:]),
                 reads=[pT], writes=[hT])

    def run(self, xt, xap, hT, col0, evac_eng="vector"):
        self.post(self.pre(xt, xap), hT, col0, evac_eng)


def load_gain_bcast(P, name, dram_t, ap1d):
    g = P.sb(name, [128, D], F32)
    P.dma("sync", g, g[:], dram_t, ap1d.partition_broadcast(128))
    return g


def build_m0(S):
    nc = bass.Bass("TRN2", target_bir_lowering=False)
    NCH = S // 512
    x = nc.dram_tensor("x", [S, D], F32, kind="ExternalInput").ap()
    gm = nc.dram_tensor("gm", [D], F32, kind="ExternalInput").ap()
    w = nc.dram_tensor("w", [D, 768], F32, kind="ExternalInput").ap()
    cw = nc.dram_tensor("cw", [128, 3], F32, kind="ExternalInput").ap()
    mixo = nc.dram_tensor("mixo", [256, S], BF16, kind="ExternalOutput").ap()
    with ExitStack() as st:
        P = Prog(nc, st)
        emit_m0(P, st, S, x, gm, w, cw, mixo)
        P.finish()
    return nc


def emit_m0(P, st, S, x, gm, w, cw, mixo, chunk_done=None):
    NCH = S // 512
    NT128 = S // 128
    xd, gmd, wd, cwd = P.dram(x, "x"), P.dram(gm, "gm"), P.dram(w, "w"), P.dram(cw, "cw")
    mixod = P.dram(mixo, "mixo") if mixo is not None else None
    c = make_consts(P)
    masks = make_masks(P, True, "msb") if not DEBUG.get("no_masks") else None
    QT = P.sb("QT", [128, S], BF16)
    KTs = P.sb("KTs", [128, S], BF16)
    KTn = P.sb("KTn", [128, S], BF16)
    Vt = P.sb("Vt", [128, NT128, 128], BF16)
    AOT = P.sb("AOT", [128, S], BF16)
    COT = P.sb("COT", [128, S], BF16)
    wt = P.sb("wt", [128, 8, 768], BF16)
    cwt = P.sb("cwt", [128, 3], F32)
    P.dma("gpsimd", wt, wt[:], wd, w.rearrange("(k p) n -> p k n", p=128))
    P.dma("sync", cwt, cwt[:], cwd, cw)
    g = load_gain_bcast(P, "gmix", gmd, gm)

    with ExitStack() as s1:
        xts = [P.sb("xt%d" % i, [128, D], F32, s1) for i in range(3)]
        hTs = [P.sb("hT%d" % i, [128, 8, 512], BF16, s1) for i in range(2)]
        pTs = [P.ps("pT%d" % i, [128, 8, 128], BF16, s1) for i in range(2)]
        pPs = [P.ps("pP%d" % i, [128, 512], F32, s1) for i in range(3)]
        pV = P.ps("pV", [128, 4, 128], F32, s1)
        Bs = [P.sb("Bs%d" % i, [128, 512], F32, s1) for i in range(2)]
        Cs = [P.sb("Cs%d" % i, [128, 512], F32, s1) for i in range(2)]
        cus = [P.sb("cu%d" % i, [128, 514], F32, s1) for i in range(2)]
        ys = [P.sb("y%d" % i, [128, 512], F32, s1) for i in range(2)]
        nt = NormTr(P, c, g, pTs, "n1", s1)
        P.op("gpsimd", lambda e: e.memset(cus[0][:, 0:2], 0.0), writes=[cus[0]])
        pi = 0
        for ch in range(NCH if not DEBUG.get("no_p1") else 0):
            hT = hTs[ch % 2]

            def load_pre(ti):
                xt = xts[ti % 3]
                P.dma("sync", xt, xt[:], xd, x[ti * 128:(ti + 1) * 128, :])
                return nt.pre(xt, xt[:])

            if ch == 0:
                nxt = load_pre(0)
            for tl in range(4):
                ti = ch * 4 + tl
                cur = nxt
                if ti + 1 < NCH * 4:
                    nxt = load_pre(ti + 1)
                nt.post(cur, hT, tl * 128, evac_eng=("vector" if tl % 2 == 0 else "scalar"))
            cs = slice(ch * 512, (ch + 1) * 512)

            def proj(j):
                nonlocal pi
                pp = pPs[pi % 3]
                pi += 1
                for k in range(8):
                    P.op("tensor", lambda e, k=k, pp=pp: e.matmul(pp[:], lhsT=wt[:, k, j * 128:(j + 1) * 128],
                                                                  rhs=hT[:, k, :], start=(k == 0), stop=(k == 7)),
                         reads=[wt, hT], writes=[pp], signal=(k == 7))
                return pp

            if DEBUG.get("no_proj"):
                continue
            pp = proj(0)
            P.op("scalar", lambda e, pp=pp: e.copy(out=QT[:, cs], in_=pp[:]), reads=[pp], writes=[QT])
            if DEBUG.get("only_q"):
                continue
            pp = proj(1)
            if not DEBUG.get("k_dve_only"):
                P.op("scalar", lambda e, pp=pp: e.activation(out=KTs[:, cs], in_=pp[:], func=AF.Copy, scale=0.125),
                     reads=[pp], writes=[KTs])
            if not DEBUG.get("k_act_only"):
              P.op("vector", lambda e, pp=pp: e.tensor_scalar(out=KTn[:, cs], in0=pp[:], scalar1=-0.125, scalar2=None,
                                                            op0=ALU.mult), reads=[pp], writes=[KTn])
            if DEBUG.get("only_qk"):
                continue
            for tl in range(4):
                for k in range(8):
                    P.op("tensor", lambda e, k=k, tl=tl: e.matmul(pV[:, tl, :], lhsT=hT[:, k, tl * 128:(tl + 1) * 128],
                                                                  rhs=wt[:, k, 256:384], start=(k == 0), stop=(k == 7)),
                         reads=[hT, wt], writes=[pV], signal=(k == 7 and tl == 3))
            P.op("vector", lambda e: e.tensor_copy(out=Vt[:, ch * 4:(ch + 1) * 4, :], in_=pV[:]), reads=[pV], writes=[Vt])
            if DEBUG.get("no_conv"):
                continue
            Bsb, Csb, cu, cun, y = Bs[ch % 2], Cs[ch % 2], cus[ch % 2], cus[(ch + 1) % 2], ys[ch % 2]
            pp = proj(3)
            P.op("scalar", lambda e, pp=pp: e.copy(out=Bsb[:], in_=pp[:]), reads=[pp], writes=[Bsb])
            pp = proj(4)
            P.op("scalar", lambda e, pp=pp: e.copy(out=Csb[:], in_=pp[:]), reads=[pp], writes=[Csb])
            pp = proj(5)
            P.op("vector", lambda e, pp=pp: e.tensor_tensor(out=cu[:, 2:514], in0=pp[:], in1=Csb[:], op=ALU.mult),
                 reads=[pp, Csb], writes=[cu])
            P.op("vector", lambda e: e.tensor_scalar(out=y[:], in0=cu[:, 0:512], scalar1=cwt[:, 0:1], scalar2=None,
                                                     op0=ALU.mult), reads=[cu, cwt], writes=[y])
            P.op("vector", lambda e: e.scalar_tensor_tensor(out=y[:], in0=cu[:, 1:513], scalar=cwt[:, 1:2], in1=y[:],
                                                            op0=ALU.mult, op1=ALU.add), reads=[cu, cwt, y], writes=[y])
            P.op("vector", lambda e: e.scalar_tensor_tensor(out=y[:], in0=cu[:, 2:514], scalar=cwt[:, 2:3], in1=y[:],
                                                            op0=ALU.mult, op1=ALU.add), reads=[cu, cwt, y], writes=[y])
            P.op("gpsimd", lambda e: e.tensor_tensor(out=COT[:, cs], in0=y[:], in1=Bsb[:], op=ALU.mult),
                 reads=[y, Bsb], writes=[COT])
            P.op("gpsimd", lambda e: e.tensor_copy(out=cun[:, 0:2], in_=cu[:, 512:514]), reads=[cu], writes=[cun])
        P.barrier()
    if mixo is not None:
        P.dma("sync", mixod, mixo[128:256, :], COT, COT[:], disjoint=True)

    with ExitStack() as s2:
        if DEBUG.get("skip_p2"):
            NCH = 0
        S2s = P.ps("S2s", [128, 2, 512], F32, s2)
        X2s = [P.ps("X2_%d" % i, [128, 2, 512], F32, s2) for i in range(2)]
        Oh = [P.ps("O%d" % i, [128, 512], F32, s2) for i in range(2)]
        esb = [P.sb("esb%d" % i, [128, 2, 512], F32, s2) for i in range(2)]
        spb = [P.sb("spb%d" % i, [128, 2, 512], BF16, s2) for i in range(2)]
        pbf = [P.sb("pbf%d" % i, [128, 2, 512], BF16, s2) for i in range(2)]
        Rf = P.sb("Rf", [128, 2, 512], F32, s2)
        Rb = [P.sb("Rb%d" % i, [128, 2, 512], BF16, s2) for i in range(2)]
        for t_ in spb + pbf:
            P.op("gpsimd", lambda e, t_=t_: e.memset(t_[:], 0.0), writes=[t_])
        def mk(qc):
            qs = slice(qc * 512, (qc + 1) * 512)
            nj = 4 * qc + 4

            def A(j):
                for h in range(2):
                    hs = slice(h * 64, (h + 1) * 64)
                    P.op("tensor", lambda e, h=h, hs=hs: e.matmul(
                        S2s[:, h, :], lhsT=KTs[hs, j * 128:(j + 1) * 128], rhs=QT[hs, qs], start=True, stop=True),
                        reads=[KTs, QT], writes=[S2s], signal=(h == 1))

            def B(j):
                es, sp = esb[j % 2], spb[j % 2]
                c0 = max(0, j - 4 * qc) * 128
                P.op("scalar", lambda e: e.activation(out=es[:, :, c0:], in_=S2s[:, :, c0:], func=AF.Exp),
                     reads=[S2s], writes=[es])
                P.op("scalar", lambda e: e.activation(out=sp[:, :, c0:], in_=es[:, :, c0:], func=AF.Ln, bias=c["oneb"][:]),
                     reads=[es, c["oneb"]], writes=[sp])
                if j >= 4 * qc:
                    m = masks[j - 4 * qc]
                    P.op("vector", lambda e: e.tensor_tensor(out=sp[:], in0=sp[:], in1=m[:], op=ALU.mult),
                         reads=[sp, m], writes=[sp])

            def C(j):
                sp = spb[j % 2]
                rb = Rb[j % 2]
                X2 = X2s[j % 2]
                first = (j == nj - 1)
                for h in range(2):
                    hs = slice(h * 64, (h + 1) * 64)
                    P.op("tensor", lambda e, h=h: e.matmul(X2[:, h, :], lhsT=c["tri"][:], rhs=sp[:, h, :],
                                                           start=True, stop=False),
                         reads=[c["tri"], sp], writes=[X2], signal=False)
                    if not first:
                        P.op("tensor", lambda e, h=h: e.matmul(X2[:, h, :], lhsT=c["ones"][:], rhs=rb[:, h, :],
                                                               start=False, stop=False),
                             reads=[c["ones"], rb], writes=[X2], signal=False)
                    P.op("tensor", lambda e, h=h, hs=hs: e.matmul(X2[:, h, :], lhsT=KTn[hs, j * 128:(j + 1) * 128],
                                                                  rhs=QT[hs, qs], start=False, stop=True),
                         reads=[KTn, QT], writes=[X2], signal=(h == 1))

            def Dd(j):
                if j == 0:
                    return
                sp = spb[j % 2]
                rbn = Rb[(j - 1) % 2]
                if j == nj - 1:
                    P.op("vector", lambda e: e.tensor_copy(out=Rf[:], in_=sp[:]), reads=[sp], writes=[Rf])
                    P.op("vector", lambda e: e.tensor_copy(out=rbn[:], in_=sp[:]), reads=[sp], writes=[rbn])
                else:
                    P.op("vector", lambda e: e.tensor_tensor(out=Rf[:], in0=Rf[:], in1=sp[:], op=ALU.add),
                         reads=[Rf, sp], writes=[Rf])
                    P.op("vector", lambda e: e.tensor_copy(out=rbn[:], in_=Rf[:]), reads=[Rf], writes=[rbn])

            def E(j):
                pb = pbf[j % 2]
                X2 = X2s[j % 2]
                c0 = max(0, j - 4 * qc) * 128
                P.op("scalar", lambda e: e.activation(out=pb[:, :, c0:], in_=X2[:, :, c0:], func=AF.Exp, scale=-1.0),
                     reads=[X2], writes=[pb])
                if j >= 4 * qc:
                    m = masks[j - 4 * qc]
                    P.op("vector", lambda e: e.tensor_tensor(out=pb[:], in0=pb[:], in1=m[:], op=ALU.mult),
                         reads=[pb, m], writes=[pb])

            def Ff(j):
                pb = pbf[j % 2]
                for h in range(2):
                    P.op("tensor", lambda e, h=h: e.matmul(Oh[h][:], lhsT=Vt[:, j, :], rhs=pb[:, h, :],
                                                           start=(j == nj - 1), stop=(j == 0)),
                         reads=[Vt, pb], writes=[Oh[h]], signal=True)

            def prologue():
                A(nj - 1)
                B(nj - 1)
                if nj >= 2:
                    A(nj - 2)

            return dict(A=A, B=B, C=C, Dd=Dd, E=E, Ff=Ff, qs=qs, nj=nj, prologue=prologue)

        blks = [mk(qc) for qc in range(NCH)]
        if blks:
            blks[0]["prologue"]()
        for qc in range(NCH):
            b = blks[qc]
            A, B, C, Dd, E, Ff, qs, nj = b["A"], b["B"], b["C"], b["Dd"], b["E"], b["Ff"], b["qs"], b["nj"]
            for j in range(nj - 1, -1, -1):
                C(j)
                if j + 1 <= nj - 1:
                    Ff(j + 1)
                Dd(j)
                if j - 1 >= 0:
                    B(j - 1)
                if j - 2 >= 0:
                    A(j - 2)
                E(j)
            if qc + 1 < NCH:
                blks[qc + 1]["prologue"]()
            Ff(0)
            for h in range(2):
                hs = slice(h * 64, (h + 1) * 64)
                if h == 0:
                    P.op("vector", lambda e, hs=hs, h=h: e.tensor_copy(out=AOT[hs, qs], in_=Oh[h][hs, :]),
                         reads=[Oh[h]], writes=[AOT])
                else:
                    P.op("scalar", lambda e, hs=hs, h=h: e.copy(out=AOT[hs, qs], in_=Oh[h][hs, :]),
                         reads=[Oh[h]], writes=[AOT])
            if chunk_done is not None:
                chunk_done(qc, qs, [AOT, COT])
        P.barrier()
    if mixo is not None:
        if not DEBUG.get("skip_p2"):
            P.dma("sync", mixod, mixo[0:128, :], AOT, AOT[:], disjoint=True)
        P.wait_all("sync", [mixod])


def build_dense(NT, final):
    nc = bass.Bass("TRN2", target_bir_lowering=False)
    mixT = nc.dram_tensor("mixT", [D, NT], BF16, kind="ExternalInput").ap()
    x = nc.dram_tensor("x", [NT, D], F32, kind="ExternalInput").ap()
    wo = nc.dram_tensor("wo", [D, D], F32, kind="ExternalInput").ap()
    g2 = nc.dram_tensor("g2", [D], F32, kind="ExternalInput").ap()
    wu = nc.dram_tensor("wu", [D, DFF], F32, kind="ExternalInput").ap()
    wdn = nc.dram_tensor("wdn", [DFF, D], F32, kind="ExternalInput").ap()
    g3 = nc.dram_tensor("g3", [D], F32, kind="ExternalInput").ap()
    if final:
        y = nc.dram_tensor("y", [NT, D], F32, kind="ExternalOutput").ap()
        hTo = None
    else:
        y = nc.dram_tensor("xo", [NT, D], F32, kind="ExternalOutput").ap()
        hTo = nc.dram_tensor("hTo", [D, NT], BF16, kind="ExternalOutput").ap()
    with ExitStack() as st:
        P = Prog(nc, st)
        emit_dense(P, st, NT, final, mixT, x, wo, g2, wu, wdn, g3, y, hTo)
        P.finish()
    return nc


def emit_dense(P, st, NT, final, mixT, x, wo, g2, wu, wdn, g3, y, hTo, load_mixT=None, ht_done=None):
    NTL = NT // 128
    NCH = NT // 512
    mixTd, xd, wod, g2d, wud, wdnd, g3d, yd = (P.dram(mixT, "mixT") if mixT is not None else None, x if isinstance(x, T) else P.dram(x, "x"), P.dram(wo, "wo"),
                                                P.dram(g2, "g2"), P.dram(wu, "wu"), P.dram(wdn, "wdn"),
                                                P.dram(g3, "g3"), y if isinstance(y, T) else P.dram(y, "y"))
    if isinstance(x, T):
        x = x.t
    if isinstance(y, T):
        y = y.t
    hTod = P.dram(hTo, "hTo") if hTo is not None else None
    c = make_consts(P)
    gg2 = load_gain_bcast(P, "gg2", g2d, g2)
    gg3 = load_gain_bcast(P, "gg3", g3d, g3)
    X = [P.sb("X%d" % i, [128, D], F32) for i in range(NTL)]
    h2T = P.sb("h2T", [128, 8, NT], BF16)
    for g0 in range(0, NTL, 4):
        grp = X[g0:g0 + 4]
        for i in range(g0, min(g0 + 4, NTL)):
            P.dma("sync", X[i], X[i][:], xd, x[i * 128:(i + 1) * 128, :], owner=grp[0])
        P.group_done(grp[0], grp)
    pTs = [P.ps("pT%d" % i, [128, 8, 128], BF16) for i in range(2)]
    pA = [P.ps("pA%d" % i, [128, 512], F32) for i in range(4)]
    wuq0 = P.sb("wuq0", [128, 8, 1024], BF16)
    wdq0 = P.sb("wdq0", [128, 8, D], BF16)
    P.dma("gpsimd", wuq0, wuq0[:], wud, wu[:, 0:1024].rearrange("(k p) n -> p k n", p=128))
    P.dma("gpsimd", wdq0, wdq0[:], wdnd, wdn[0:1024, :].rearrange("(k p) n -> p k n", p=128))
    with ExitStack() as s1:
        mt = P.sb("mixTs", [128, 8, NT], BF16, s1)
        wot = P.sb("wot", [128, 8, D], BF16, s1)
        if load_mixT is not None:
            load_mixT(mt, s1)
        else:
            P.dma("sync", mt, mt[:], mixTd, mixT.rearrange("(k p) n -> p k n", p=128))
        P.dma("gpsimd", wot, wot[:], wod, wo.rearrange("(k p) n -> p k n", p=128))
        nt = NormTr(P, c, gg2, pTs, "n2", s1)
        hprev = None
        for i in range(NTL):
            for hf in range(2):
                pp = pA[(i * 2 + hf) % 4]
                for k in range(8):
                    P.op("tensor", lambda e, k=k, pp=pp, hf=hf, i=i: e.matmul(
                        pp[:], lhsT=mt[:, k, i * 128:(i + 1) * 128], rhs=wot[:, k, hf * 512:(hf + 1) * 512],
                        start=(k == 0), stop=(k == 7)), reads=[mt, wot], writes=[pp], signal=(k == 7))
                P.op("vector", lambda e, pp=pp, hf=hf, i=i: e.tensor_tensor(
                    out=X[i][:, hf * 512:(hf + 1) * 512], in0=pp[:], in1=X[i][:, hf * 512:(hf + 1) * 512], op=ALU.add),
                    reads=[pp, X[i]], writes=[X[i]])
            hcur = nt.pre(X[i], X[i][:])
            if hprev is not None:
                nt.post(hprev[0], h2T, hprev[1] * 128, evac_eng=("vector" if hprev[1] % 2 == 0 else "scalar"))
            hprev = (hcur, i)
        nt.post(hprev[0], h2T, hprev[1] * 128, evac_eng=("vector" if hprev[1] % 2 == 0 else "scalar"))
        P.barrier()
    with ExitStack() as s2:
        wuq = [wuq0, P.sb("wuq1", [128, 8, 1024], BF16, s2)]
        wdq = [wdq0, P.sb("wdq1", [128, 8, D], BF16, s2)]
        aT = [P.sb("aT%d" % i, [128, 8, 512], BF16, s2) for i in range(2)]
        rl = [P.sb("rl%d" % i, [128, 512], F32, s2) for i in range(2)]
        it = 0
        ri = 0
        for q in range(4):
            wuqt, wdqt = wuq[q % 2], wdq[q % 2]
            if q > 0:
                P.dma("gpsimd", wuqt, wuqt[:], wud, wu[:, q * 1024:(q + 1) * 1024].rearrange("(k p) n -> p k n", p=128))
                P.dma("gpsimd", wdqt, wdqt[:], wdnd, wdn[q * 1024:(q + 1) * 1024, :].rearrange("(k p) n -> p k n", p=128))
            for ch in range(NCH):
                a = aT[it % 2]
                it += 1
                ts = slice(ch * 512, (ch + 1) * 512)
                for m in range(8):
                    pp = pA[m % 2]
                    for k in range(8):
                        P.op("tensor", lambda e, k=k, m=m, pp=pp: e.matmul(
                            pp[:], lhsT=wuqt[:, k, m * 128:(m + 1) * 128], rhs=h2T[:, k, ts],
                            start=(k == 0), stop=(k == 7)), reads=[wuqt, h2T], writes=[pp], signal=(k == 7))
                    r = rl[ri % 2]
                    ri += 1
                    P.op("scalar", lambda e, pp=pp, r=r: e.activation(out=r[:], in_=pp[:], func=AF.Relu),
                         reads=[pp], writes=[r])
                    P.op("vector", lambda e, r=r, m=m, a=a: e.tensor_tensor(out=a[:, m, :], in0=r[:], in1=r[:], op=ALU.mult),
                         reads=[r], writes=[a])
                for tl in range(4):
                    i = ch * 4 + tl
                    for hf in range(2):
                        pp = pA[2 + (tl * 2 + hf) % 2]
                        for m in range(8):
                            P.op("tensor", lambda e, m=m, pp=pp, hf=hf, tl=tl, a=a: e.matmul(
                                pp[:], lhsT=a[:, m, tl * 128:(tl + 1) * 128], rhs=wdqt[:, m, hf * 512:(hf + 1) * 512],
                                start=(m == 0), stop=(m == 7)), reads=[a, wdqt], writes=[pp], signal=(m == 7))
                        P.op("vector", lambda e, pp=pp, hf=hf, i=i: e.tensor_tensor(
                            out=X[i][:, hf * 512:(hf + 1) * 512], in0=pp[:], in1=X[i][:, hf * 512:(hf + 1) * 512],
                            op=ALU.add), reads=[pp, X[i]], writes=[X[i]])
        P.barrier()
    if final:
        nt3 = NormTr(P, c, gg3, pTs, "n3")
        outs = [P.sb("yo%d" % i, [128, D], F32) for i in range(2)]
        for i in range(NTL):
            rs = nt3.stats(X[i], X[i][:])
            nt3.i += 1
            o = outs[i % 2]
            P.op("vector", lambda e, i=i, o=o, rs=rs: e.scalar_tensor_tensor(
                out=o[:], in0=X[i][:], scalar=rs[:], in1=gg3[:], op0=ALU.mult, op1=ALU.mult),
                reads=[X[i], rs, gg3], writes=[o])
            P.dma("sync", yd, y[i * 128:(i + 1) * 128, :], o, o[:], disjoint=True)
        P.wait_all("sync", [yd])
    else:
        nt3 = NormTr(P, c, gg3, pTs, "n3")
        hT3 = P.sb("hT3", [128, 8, NT], BF16)
        hprev = None
        for i in range(NTL):
            P.dma("sync", yd, y[i * 128:(i + 1) * 128, :], X[i], X[i][:], disjoint=True)
            hcur = nt3.pre(X[i], X[i][:])
            if hprev is not None:
                nt3.post(hprev[0], hT3, hprev[1] * 128, evac_eng=("vector" if hprev[1] % 2 == 0 else "scalar"))
            hprev = (hcur, i)
        nt3.post(hprev[0], hT3, hprev[1] * 128, evac_eng=("vector" if hprev[1] % 2 == 0 else "scalar"))
        if ht_done is not None:
            ht_done(hT3)
        else:
            P.dma("sync", hTod, hTo.rearrange("(k p) n -> p k n", p=128), hT3, hT3[:])
            P.wait_all("sync", [yd, hTod])


def build_m1(S, lam_init):
    nc = bass.Bass("TRN2", target_bir_lowering=False)
    hT = nc.dram_tensor("hT", [D, S], BF16, kind="ExternalInput").ap()
    w = nc.dram_tensor("w", [D, 768], F32, kind="ExternalInput").ap()
    lamv = nc.dram_tensor("lamv", [4, 64], F32, kind="ExternalInput").ap()
    sg = nc.dram_tensor("sg", [128, 1], F32, kind="ExternalInput").ap()
    mixo = nc.dram_tensor("mixo", [256, S], BF16, kind="ExternalOutput").ap()
    with ExitStack() as st:
        P = Prog(nc, st)
        emit_m1(P, st, S, lam_init, hT, w, lamv, sg, mixo)
        P.finish()
    return nc


def emit_m1(P, st, S, lam_init, hT, w, lamv, sg, mixo, ht_src=None, ch_order=None, chunk_done=None):
    NCH = S // 512
    NT128 = S // 128
    hTd = P.dram(hT, "hT") if hT is not None else None
    mixod = P.dram(mixo, "mixo") if mixo is not None else None
    wd, lamd, sgd = P.dram(w, "w"), P.dram(lamv, "lamv"), P.dram(sg, "sg")
    c = make_consts(P)
    masks = make_masks(P, False, "mdf")
    QT = [P.sb("QT%d" % h, [128, S], BF16) for h in range(2)]
    KT = [P.sb("KT%d" % h, [128, S], BF16) for h in range(2)]
    Vt = P.sb("Vt", [128, NT128, 256], BF16)
    AOT = [P.sb("AOT%d" % h, [128, S], BF16) for h in range(2)]
    wt = P.sb("wt", [128, 8, 768], BF16)
    P.dma("gpsimd", wt, wt[:], wd, w.rearrange("(k p) n -> p k n", p=128))
    lt = P.sb("lt", [128, 4, 64], F32)
    P.dma("sync", lt, lt[:], lamd, lamv.partition_broadcast(128))
    lp = P.sb("lp", [128, 2, 64], F32)
    ls = P.sb("ls", [128, 2], F32)
    le = P.sb("le", [128, 2], F32)
    nlam = P.sb("nlam", [128, 1], F32)
    sgt = P.sb("sgt", [128, 1], F32)
    sgs = P.sb("sgs", [128, 1], F32)
    P.dma("sync", sgt, sgt[:], sgd, sg)
    P.op("vector", lambda e: e.tensor_tensor(out=lp[:, 0, :], in0=lt[:, 0, :], in1=lt[:, 1, :], op=ALU.mult),
         reads=[lt], writes=[lp])
    P.op("vector", lambda e: e.tensor_tensor(out=lp[:, 1, :], in0=lt[:, 2, :], in1=lt[:, 3, :], op=ALU.mult),
         reads=[lt, lp], writes=[lp])
    P.op("vector", lambda e: e.reduce_sum(out=ls[:], in_=lp[:], axis=mybir.AxisListType.X), reads=[lp], writes=[ls])
    P.op("scalar", lambda e: e.activation(out=le[:], in_=ls[:], func=AF.Exp), reads=[ls], writes=[le])
    P.op("vector", lambda e: e.scalar_tensor_tensor(out=nlam[:], in0=le[:, 1:2], scalar=-lam_init, in1=le[:, 0:1],
                                                    op0=ALU.add, op1=ALU.subtract), reads=[le], writes=[nlam])
    P.op("vector", lambda e: e.tensor_scalar(out=sgs[:], in0=sgt[:], scalar1=1.0 - lam_init, scalar2=None, op0=ALU.mult),
         reads=[sgt], writes=[sgs])

    with ExitStack() as s1:
        hts = [P.sb("hTc%d" % i, [128, 8, 512], BF16, s1) for i in range(2)]
        pPs = [P.ps("pP%d" % i, [128, 512], F32, s1) for i in range(4)]
        pV = P.ps("pV", [128, 4, 256], F32, s1)
        pi = 0
        for ci, ch in enumerate(ch_order if ch_order is not None else range(NCH)):
            ht = hts[ci % 2]
            cs = slice(ch * 512, (ch + 1) * 512)
            if ht_src is not None:
                src_t, src_ap = ht_src(ch)
                P.dma("sync", ht, ht[:], src_t, src_ap)
            else:
                P.dma("sync", ht, ht[:], hTd, hT[:, cs].rearrange("(k p) n -> p k n", p=128))
            for j in range(4):
                pp = pPs[pi % 4]
                pi += 1
                for k in range(8):
                    P.op("tensor", lambda e, k=k, pp=pp, j=j: e.matmul(pp[:], lhsT=wt[:, k, j * 128:(j + 1) * 128],
                                                                       rhs=ht[:, k, :], start=(k == 0), stop=(k == 7)),
                         reads=[wt, ht], writes=[pp], signal=(k == 7))
                dst = (QT[0], QT[1], KT[0], KT[1])[j]
                if j % 2 == 0:
                    P.op("scalar", lambda e, pp=pp, dst=dst: e.copy(out=dst[:, cs], in_=pp[:]), reads=[pp], writes=[dst])
                else:
                    P.op("vector", lambda e, pp=pp, dst=dst: e.tensor_copy(out=dst[:, cs], in_=pp[:]), reads=[pp], writes=[dst])
            for tl in range(4):
                for k in range(8):
                    P.op("tensor", lambda e, k=k, tl=tl: e.matmul(pV[:, tl, :], lhsT=ht[:, k, tl * 128:(tl + 1) * 128],
                                                                  rhs=wt[:, k, 512:768], start=(k == 0), stop=(k == 7)),
                         reads=[ht, wt], writes=[pV], signal=(k == 7 and tl == 3))
            P.op("vector", lambda e: e.tensor_copy(out=Vt[:, ch * 4:(ch + 1) * 4, :], in_=pV[:]), reads=[pV], writes=[Vt])
        P.barrier()

    with ExitStack() as s2:
        S2 = [P.ps("S2_%d" % i, [128, 2, 512], F32, s2) for i in range(2)]
        O2 = P.ps("O2", [128, 2, 512], F32, s2)
        L2t = P.ps("L2", [128, 2, 512], F32, s2)
        L2a = T(L2t[:, 0, :], "L2a")
        L2b = T(L2t[:, 1, :], "L2b")
        L2a.psum = L2b.psum = True
        lnl = P.sb("lnl", [128, 2, 512], F32, s2)
        pbf = [P.sb("pbf%d" % i, [128, 2, 512], BF16, s2) for i in range(3)]
        rl = P.sb("rl", [128, 2, 512], F32, s2)
        acc = P.sb("acc", [128, 512], F32, s2)
        ahi = P.sb("ahi", [128, 512], BF16, s2)
        alo = P.sb("alo", [128, 512], BF16, s2)
        on = P.sb("on", [128, 2, 512], F32, s2)
        od = P.sb("od", [128, 512], F32, s2)
        sq = P.sb("sq", [128, 512], BF16, s2)
        lnv = P.sb("lnv", [128, 512], F32, s2)
        rsv = P.sb("rsv", [128, 512], F32, s2)
        for t_ in pbf:
            P.op("gpsimd", lambda e, t_=t_: e.memset(t_[:], 0.0), writes=[t_])
        pending = [None]
        first_a_done = [False]
        for qc in range(NCH):
            qs = slice(qc * 512, (qc + 1) * 512)
            nj = 4 * qc + 4
            for h in range(2):
                def A_blk(qc_, h_, j):
                    sb_ = S2[j % 2]
                    qs_ = slice(qc_ * 512, (qc_ + 1) * 512)
                    for m in range(2):
                        ms_ = slice(m * 64, (m + 1) * 64)
                        P.op("tensor", lambda e, m=m, ms_=ms_, sb_=sb_: e.matmul(
                            sb_[:, m, :], lhsT=KT[h_][ms_, j * 128:(j + 1) * 128], rhs=QT[h_][ms_, qs_],
                            start=True, stop=True), reads=[KT[h_], QT[h_]], writes=[sb_], signal=(m == 1))

                def A(j):
                    A_blk(qc, h, j)

                nxt_blk = (qc, 1) if h == 0 else ((qc + 1, 0) if qc + 1 < NCH else None)

                def E(j):
                    sb_, pb = S2[j % 2], pbf[j % 3]
                    c0 = max(0, j - 4 * qc) * 128
                    P.op("scalar", lambda e: e.activation(out=pb[:, :, c0:], in_=sb_[:, :, c0:], func=AF.Exp, scale=0.125),
                         reads=[sb_], writes=[pb])
                    if j >= 4 * qc:
                        m = masks[j - 4 * qc]
                        P.op("vector", lambda e: e.tensor_tensor(out=pb[:], in0=pb[:], in1=m[:], op=ALU.mult),
                             reads=[pb, m], writes=[pb])

                def Ff(j):
                    pb = pbf[j % 3]
                    st_, sp_ = (j == nj - 1), (j == 0)
                    for m in range(2):
                        P.op("tensor", lambda e, m=m: e.matmul(O2[:, m, :], lhsT=Vt[:, j, h * 128:(h + 1) * 128],
                                                               rhs=pb[:, m, :], start=st_, stop=sp_),
                             reads=[Vt, pb], writes=[O2], signal=(m == 1))
                    P.op("tensor", lambda e: e.matmul(L2a[:], lhsT=c["ones"][:], rhs=pb[:, 0, :], start=st_, stop=sp_),
                         reads=[c["ones"], pb], writes=[L2a])
                    if st_:
                        P.op("vector", lambda e: e.tensor_copy(out=acc[:], in_=pb[:, 1, :]), reads=[pb], writes=[acc])
                    else:
                        P.op("vector", lambda e: e.tensor_tensor(out=acc[:], in0=acc[:], in1=pb[:, 1, :], op=ALU.add),
                             reads=[acc, pb], writes=[acc])

                def epi1():
                    P.op("vector", lambda e: e.tensor_copy(out=ahi[:], in_=acc[:]), reads=[acc], writes=[ahi])
                    P.op("vector", lambda e: e.tensor_tensor(out=alo[:], in0=acc[:], in1=ahi[:], op=ALU.subtract),
                         reads=[acc, ahi], writes=[alo])
                    P.op("tensor", lambda e: e.matmul(L2b[:], lhsT=c["ones"][:], rhs=ahi[:], start=True, stop=False),
                         reads=[c["ones"], ahi], writes=[L2b], signal=False)
                    P.op("tensor", lambda e: e.matmul(L2b[:], lhsT=c["ones"][:], rhs=alo[:], start=False, stop=True),
                         reads=[c["ones"], alo], writes=[L2b])

                def epi1b():
                    P.op("scalar", lambda e: e.activation(out=lnl[:], in_=L2t[:], func=AF.Ln), reads=[L2a, L2b], writes=[lnl])
                    P.op("scalar", lambda e: e.activation(out=rl[:], in_=lnl[:], func=AF.Exp, scale=-1.0), reads=[lnl], writes=[rl])
                    P.op("vector", lambda e: e.tensor_tensor(out=on[:], in0=O2[:], in1=rl[:], op=ALU.mult),
                         reads=[O2, rl], writes=[on])

                def epi2(h=h, qs=qs, qc=qc):
                    P.op("vector", lambda e: e.scalar_tensor_tensor(out=od[:], in0=on[:, 1, :], scalar=nlam[:], in1=on[:, 0, :],
                                                                    op0=ALU.mult, op1=ALU.add), reads=[on, nlam], writes=[od])
                    P.op("gpsimd", lambda e: e.tensor_tensor(out=sq[:], in0=od[:], in1=od[:], op=ALU.mult), reads=[od], writes=[sq])
                    P.op("tensor", lambda e: e.matmul(L2b[:], lhsT=c["ones"][:], rhs=sq[:], start=True, stop=True),
                         reads=[c["ones"], sq], writes=[L2b])
                    P.op("scalar", lambda e: e.activation(out=lnv[:], in_=L2b[:], func=AF.Ln, scale=1.0 / 128,
                                                          bias=c["epsb"][:]), reads=[L2b, c["epsb"]], writes=[lnv])
                    P.op("scalar", lambda e: e.activation(out=rsv[:], in_=lnv[:], func=AF.Exp, scale=-0.5), reads=[lnv], writes=[rsv])
                    P.op("vector", lambda e: e.scalar_tensor_tensor(out=AOT[h][:, qs], in0=od[:], scalar=sgs[:], in1=rsv[:],
                                                                    op0=ALU.mult, op1=ALU.mult),
                         reads=[od, sgs, rsv], writes=[AOT[h]])
                    if h == 1 and chunk_done is not None:
                        chunk_done(qc, qs, AOT)

                if not first_a_done[0]:
                    A(nj - 1)
                A(nj - 2)
                if pending[0] is not None:
                    pending[0][0]()
                E(nj - 1)
                E(nj - 2)
                A(nj - 3)
                pre_a, pre_e = {nj - 1, nj - 2, nj - 3}, {nj - 1, nj - 2}
                if pending[0] is not None:
                    pending[0][1]()
                for j in range(nj - 1, -1, -1):
                    if j not in pre_e:
                        E(j)
                    if j - 2 >= 0 and (j - 2) not in pre_a:
                        A(j - 2)
                    if j == 0:
                        first_a_done[0] = nxt_blk is not None
                        if nxt_blk is not None:
                            A_blk(nxt_blk[0], nxt_blk[1], 4 * nxt_blk[0] + 3)
                    Ff(j)
                    if j == nj - 4 and pending[0] is not None:
                        pending[0][2]()
                        pending[0] = None
                pending[0] = (epi1, epi1b, epi2)
        pending[0][0]()
        pending[0][1]()
        pending[0][2]()
        P.barrier()
    if mixo is not None:
        for h in range(2):
            P.dma("sync", mixod, mixo[h * 128:(h + 1) * 128, :], AOT[h], AOT[h][:], disjoint=True)
        P.wait_all("sync", [mixod])


GROUPS = [[0, 1, 2, 3], [4, 5, 6, 7]]


def build_fused(S, lam_init):
    nc = bass.Bass("TRN2", target_bir_lowering=False)
    NT, NCH = S // 4, S // 512
    CPR = NT // 512

    def inp(n, shp, d=F32):
        return nc.dram_tensor(n, shp, d, kind="ExternalInput").ap()

    x_all, x_own = inp("x_all", [S, D]), inp("x_own", [NT, D])
    gm0, w0, cw, sel = inp("gm0", [D]), inp("w0", [D, 768]), inp("cw", [128, 3]), inp("sel", [128, 4])
    wo0, g20, wu0, wdn0, gm1 = inp("wo0", [D, D]), inp("g20", [D]), inp("wu0", [D, DFF]), inp("wdn0", [DFF, D]), inp("gm1", [D])
    w1, lamv, sg = inp("w1", [D, 768]), inp("lamv", [4, 64]), inp("sg", [128, 1])
    wo1, g21, wu1, wdn1, gfin = inp("wo1", [D, D]), inp("g21", [D]), inp("wu1", [D, DFF]), inp("wdn1", [DFF, D]), inp("gfin", [D])
    y = nc.dram_tensor("y", [NT, D], F32, kind="ExternalOutput").ap()
    src1 = nc.dram_tensor("src1", [NCH, 256, 512], BF16).ap()
    dst1 = nc.dram_tensor("dst1", [NCH, 1024, 512], BF16).ap()
    src3 = nc.dram_tensor("src3", [NCH, 256, 512], BF16).ap()
    dst3 = nc.dram_tensor("dst3", [NCH, 1024, 512], BF16).ap()
    src2 = nc.dram_tensor("src2", [CPR, 1024, 512], BF16).ap()
    dst2 = nc.dram_tensor("dst2", [CPR, 4096, 512], BF16).ap()
    xres = nc.dram_tensor("xres", [NT, D], F32).ap()
    with ExitStack() as st:
        P = Prog(nc, st)
        make_consts(P)
        scratch = P.sb("scr", [128, 8], F32)
        selt = P.sb("selt", [128, 4], F32)
        P.dma("sync", selt, selt[:], P.dram(sel, "sel"), sel)
        xresT = P.dram(xres, "xres")
        src1T = [P.dram(src1, "src1")] * NCH
        dst1T = [P.dram(dst1, "dst1")] * NCH
        src3T = [P.dram(src3, "src3")] * NCH
        dst3T = [P.dram(dst3, "dst3")] * NCH
        src2T = [P.dram(src2[i], "src2_%d" % i) for i in range(CPR)]
        dst2T = [P.dram(dst2[i], "dst2_%d" % i) for i in range(CPR)]

        def stage(prefix, fn):
            with ExitStack() as ss:
                old = P.stack
                P.stack = ss
                P.prefix = prefix
                fn(ss)
                P.drain(scratch)
                P.stack = old

        def m0_chunk_done(qc, qs, tiles):
            AOT, COT = tiles
            P.dma("sync", src1T[qc], src1[qc, 0:128, :], AOT, AOT[:, qs])
            P.dma("sync", src1T[qc], src1[qc, 128:256, :], COT, COT[:, qs])
            P.cc_allgather(dst1T[qc], dst1[qc], src1T[qc], src1[qc], GROUPS)

        def m1_chunk_done(qc, qs, AOT):
            for h in range(2):
                P.dma("sync", src3T[qc], src3[qc, h * 128:(h + 1) * 128, :], AOT[h], AOT[h][:, qs])
            P.cc_allgather(dst3T[qc], dst3[qc], src3T[qc], src3[qc], GROUPS)

        def make_load_mixT(dstT, dst):
            def load(mt, sstack):
                cands = [P.sb("cand%d" % i, [128, 2, NT], BF16, sstack) for i in range(2)]
                ci = 0
                for g in range(4):
                    for r in range(4):
                        cd = cands[ci % 2]
                        ci += 1
                        for h in range(2):
                            P.dma("sync", cd, cd[:, h, :].rearrange("p (q c) -> p q c", q=CPR), [dstT[0]],
                                  dst[r * CPR:(r + 1) * CPR, g * 256 + h * 128:g * 256 + (h + 1) * 128, :].rearrange("q p c -> p q c"),
                                  disjoint=(h == 1))
                        mo = mt[:, 2 * g:2 * g + 2, :]
                        if r == 0:
                            P.op("vector", lambda e: e.tensor_scalar(out=mo, in0=cd[:], scalar1=selt[:, 0:1], scalar2=None,
                                                                     op0=ALU.mult), reads=[cd, selt], writes=[mt])
                        else:
                            P.op("vector", lambda e: e.scalar_tensor_tensor(out=mo, in0=cd[:], scalar=selt[:, r:r + 1], in1=mo,
                                                                            op0=ALU.mult, op1=ALU.add),
                                 reads=[cd, selt, mt], writes=[mt])
            return load

        def d0_ht_done(hT3):
            for i in range(CPR):
                P.dma("sync", src2T[i], src2[i].rearrange("(k p) n -> p k n", p=128), hT3, hT3[:, :, i * 512:(i + 1) * 512])
                P.cc_allgather(dst2T[i], dst2[i], src2T[i], src2[i], GROUPS)

        def m1_ht_src(ch):
            r, i = ch // CPR, ch % CPR
            return dst2T[i], dst2[i, r * 1024:(r + 1) * 1024, :].rearrange("(k p) n -> p k n", p=128)

        stage("a_", lambda ss: emit_m0(P, ss, S, x_all, gm0, w0, cw, None, chunk_done=m0_chunk_done))
        stage("b_", lambda ss: emit_dense(P, ss, NT, False, None, x_own, wo0, g20, wu0, wdn0, gm1, xresT, None,
                                          load_mixT=make_load_mixT(dst1T, dst1), ht_done=d0_ht_done))
        stage("c_", lambda ss: emit_m1(P, ss, S, lam_init, None, w1, lamv, sg, None, ht_src=m1_ht_src,
                                       ch_order=[r * CPR + i for i in range(CPR) for r in range(4)],
                                       chunk_done=m1_chunk_done))
        stage("d_", lambda ss: emit_dense(P, ss, NT, True, None, xresT, wo1, g21, wu1, wdn1, gfin, y, None,
                                          load_mixT=make_load_mixT(dst3T, dst3)))
        P.finish()
        print("fused program: ninst", P.ninst, "nwaits", P.nwaits)
    return nc


def fused_inputs(x, norm_mix, norm_mlp, norm_final, w_in_even, conv_w, w_out_even, w_in_odd,
                 lamv, subln_g, w_out_odd, w_up, w_down):
    B, S, _ = x.shape
    NT = S // 4
    perm = np.concatenate([np.concatenate([np.arange(g * 128, (g + 1) * 128), 512 + np.arange(g * 128, (g + 1) * 128)])
                           for g in range(4)])
    wo0p = np.ascontiguousarray(w_out_even[0][perm])
    maps = []
    for c in range(B * 4):
        b, g = c // 4, c % 4
        cols0 = np.concatenate([np.arange(sec * 512 + g * 128, sec * 512 + (g + 1) * 128) for sec in range(6)])
        cols1 = np.concatenate([np.arange(sec * 1024 + g * 256, sec * 1024 + (g + 1) * 256) for sec in range(3)])
        sel = np.zeros((128, 4), np.float32)
        sel[:, g] = 1.0
        maps.append({
            "x_all": x[b], "x_own": np.ascontiguousarray(x[b, g * NT:(g + 1) * NT]),
            "gm0": norm_mix[0], "w0": np.ascontiguousarray(w_in_even[0][:, cols0]),
            "cw": np.ascontiguousarray(conv_w[0][:, g * 128:(g + 1) * 128].T), "sel": sel,
            "wo0": wo0p, "g20": norm_mlp[0], "wu0": w_up[0], "wdn0": w_down[0], "gm1": norm_mix[1],
            "w1": np.ascontiguousarray(w_in_odd[0][:, cols1]), "lamv": lamv,
            "sg": np.ascontiguousarray(subln_g[0].reshape(128, 1)),
            "wo1": w_out_odd[0], "g21": norm_mlp[1], "wu1": w_up[1], "wdn1": w_down[1], "gfin": norm_final,
        })
    return maps


_CACHE = {}


def _prog(key, fn):
    if key not in _CACHE:
        _CACHE[key] = fn()
    return _CACHE[key]


def _run(nc, in_maps):
    res = run_bass_kernel_spmd(nc, in_maps, core_ids=list(range(len(in_maps))))
    return res.results


def kernel(x, norm_mix, norm_mlp, norm_final, w_in_even, conv_w, w_out_even, w_in_odd,
           lam_q1, lam_k1, lam_q2, lam_k2, subln_g, w_out_odd, w_up, w_down):
    f32 = lambda a: np.ascontiguousarray(np.asarray(a, dtype=np.float32))
    x = f32(x)
    B, S, _ = x.shape
    NT = S // 4
    nco = B * 4
    norm_mix, norm_mlp, norm_final = f32(norm_mix), f32(norm_mlp), f32(norm_final)
    w_in_even, conv_w, w_out_even, w_in_odd = f32(w_in_even), f32(conv_w), f32(w_out_even), f32(w_in_odd)
    w_out_odd, w_up, w_down, subln_g = f32(w_out_odd), f32(w_up), f32(w_down), f32(subln_g)
    lamv = np.ascontiguousarray(np.stack([f32(lam_q1)[0], f32(lam_k1)[0], f32(lam_q2)[0], f32(lam_k2)[0]], axis=0))
    lam_init = 0.8 - 0.6 * math.exp(-0.3 * 1)

    if not DEBUG.get("unfused"):
        assert B == 2
        maps = fused_inputs(x, norm_mix, norm_mlp, norm_final, w_in_even, conv_w, w_out_even, w_in_odd,
                            lamv, subln_g, w_out_odd, w_up, w_down)
        res = _run(_prog(("fused", S), lambda: build_fused(S, lam_init)), maps)
        out = np.stack([np.concatenate([np.asarray(res[b * 4 + r]["y"]) for r in range(4)], axis=0) for b in range(B)], axis=0)
        return out.astype(np.float32)

    w0 = w_in_even[0]
    maps = []
    for c in range(nco):
        b, g = c // 4, c % 4
        cols = np.concatenate([np.arange(sec * 512 + g * 128, sec * 512 + (g + 1) * 128) for sec in range(6)])
        maps.append({"x": x[b], "gm": norm_mix[0], "w": np.ascontiguousarray(w0[:, cols]),
                     "cw": np.ascontiguousarray(conv_w[0][:, g * 128:(g + 1) * 128].T)})
    r0 = _run(_prog(("m0", S), lambda: build_m0(S)), maps)
    mixT0 = []
    for b in range(B):
        a = np.concatenate([np.asarray(r0[b * 4 + g]["mixo"])[0:128] for g in range(4)], axis=0)
        cc = np.concatenate([np.asarray(r0[b * 4 + g]["mixo"])[128:256] for g in range(4)], axis=0)
        mixT0.append(np.concatenate([a, cc], axis=0))

    maps = []
    for c in range(nco):
        b, r = c // 4, c % 4
        ts = slice(r * NT, (r + 1) * NT)
        maps.append({"mixT": np.ascontiguousarray(mixT0[b][:, ts]), "x": np.ascontiguousarray(x[b, ts]),
                     "wo": w_out_even[0], "g2": norm_mlp[0], "wu": w_up[0], "wdn": w_down[0], "g3": norm_mix[1]})
    r1 = _run(_prog(("d0", NT), lambda: build_dense(NT, False)), maps)
    hT1 = [np.ascontiguousarray(np.concatenate([np.asarray(r1[b * 4 + r]["hTo"]) for r in range(4)], axis=1))
           for b in range(B)]

    w1 = w_in_odd[0]
    maps = []
    for c in range(nco):
        b, g = c // 4, c % 4
        cols = np.concatenate([np.arange(sec * 1024 + g * 256, sec * 1024 + (g + 1) * 256) for sec in range(3)])
        maps.append({"hT": hT1[b], "w": np.ascontiguousarray(w1[:, cols]), "lamv": lamv,
                     "sg": np.ascontiguousarray(subln_g[0].reshape(128, 1))})
    r2 = _run(_prog(("m1", S), lambda: build_m1(S, lam_init)), maps)
    mixT1 = [np.concatenate([np.asarray(r2[b * 4 + g]["mixo"]) for g in range(4)], axis=0) for b in range(B)]

    maps = []
    for c in range(nco):
        b, r = c // 4, c % 4
        ts = slice(r * NT, (r + 1) * NT)
        maps.append({"mixT": np.ascontiguousarray(mixT1[b][:, ts]), "x": np.asarray(r1[c]["xo"]),
                     "wo": w_out_odd[0], "g2": norm_mlp[1], "wu": w_up[1], "wdn": w_down[1], "g3": norm_final})
    r3 = _run(_prog(("d1", NT), lambda: build_dense(NT, True)), maps)
    out = np.stack([np.concatenate([np.asarray(r3[b * 4 + r]["y"]) for r in range(4)], axis=0) for b in range(B)], axis=0)
    return out.astype(np.float32)
```

```python
import math
from contextlib import ExitStack

import numpy as np
import ml_dtypes

import concourse.bass as bass
import concourse.mybir as mybir
from concourse.bass_utils import run_bass_kernel_spmd

F32 = mybir.dt.float32
BF16 = mybir.dt.bfloat16
AF = mybir.ActivationFunctionType
ALU = mybir.AluOpType
NPBF = ml_dtypes.bfloat16

DEBUG = {}
D = 1024
DFF = 4096
EPS = 1e-6


class T:
    def __init__(self, t, name=""):
        self.t = t
        self.name = name
        self.w = {}
        self.r = {}
        self.dsem = None
        self.dcount = 0
        self.psum = False

    def __getitem__(self, idx):
        return self.t[idx]


class _Rec:
    def __init__(self):
        self.calls = []

    def __getattr__(self, name):
        def f(*a, **k):
            self.calls.append((name, a, k))
        return f


class Prog:
    ENGS = ("tensor", "vector", "scalar", "gpsimd", "sync")

    def __init__(self, nc, stack):
        self.nc = nc
        self.stack = stack
        self.ops = {e: [] for e in self.ENGS}
        self.sem = {}
        self.count = {e: 0 for e in self.ENGS}
        self.pending = {e: False for e in self.ENGS}
        self.seen = {e: {} for e in self.ENGS}
        for e in ("tensor", "vector", "scalar", "gpsimd"):
            self.sem[e] = stack.enter_context(nc.semaphore("s_" + e))
        self.ninst = 0
        self.nwaits = 0
        self.uid = 0
        self.semstack = stack
        self.prefix = ""
        self.dma_toks = {}
        self._consts = None

    def sb(self, name, shape, dt, stack=None):
        st = stack or self.stack
        name = self.prefix + name
        return T(st.enter_context(self.nc.sbuf_tensor(name, shape, dt)), name)

    def ps(self, name, shape, dt, stack=None):
        st = stack or self.stack
        name = self.prefix + name
        t = T(st.enter_context(self.nc.psum_tensor(name, shape, dt)), name)
        t.psum = True
        return t

    def dram(self, t, name=""):
        return T(t, name)

    def _dsem(self, t):
        if t.dsem is None:
            self.uid += 1
            t.dsem = self.semstack.enter_context(self.nc.semaphore("d%d_%s" % (self.uid, t.name)))
        return t.dsem

    def _deps(self, eng, reads, writes):
        need = {}
        own = self.sem.get(eng)
        for t in reads:
            for s, v in t.w.items():
                if need.get(s, 0) < v:
                    need[s] = v
            if t.psum:
                for s, v in t.r.items():
                    if s is not own and need.get(s, 0) < v:
                        need[s] = v
        for t in writes:
            for d in (t.w, t.r):
                for s, v in d.items():
                    if need.get(s, 0) < v:
                        need[s] = v
        waits = []
        seen = self.seen[eng]
        for s, v in need.items():
            if eng == "tensor" and s is self.sem["tensor"]:
                continue
            if seen.get(s, 0) < v:
                seen[s] = v
                waits.append((s, v))
        return waits

    def _post(self, tok, reads, writes):
        s, v = tok
        for t in reads:
            if t.r.get(s, 0) < v:
                t.r[s] = v
        for t in writes:
            t.w = {s: v}
            t.r = {}

    def op(self, eng, fn, reads=(), writes=(), signal=True):
        waits = self._deps(eng, reads, writes)
        sem = self.sem[eng]
        if signal:
            self.count[eng] += 1
            val = self.count[eng]
            self.pending[eng] = False
        else:
            val = self.count[eng] + 1
            self.pending[eng] = True
        self.ninst += 1
        self.nwaits += len(waits)
        rec = _Rec()
        fn(rec)
        (name, a, k), = rec.calls

        def run(e, waits=waits, name=name, a=a, k=k, signal=signal, sem=sem):
            for s, v in waits:
                e.wait_ge(s, v)
            ins = getattr(e, name)(*a, **k)
            if signal:
                ins.then_inc(sem, 1)

        self.ops[eng].append(run)
        self._post((sem, val), reads, writes)

    def dma(self, eng, out_t, out_ap, in_t, in_ap, disjoint=False, owner=None):
        in_ts = list(in_t) if isinstance(in_t, (list, tuple)) else [in_t]
        waits = self._deps(eng, in_ts, [] if disjoint else [out_t])
        so = owner if owner is not None else out_t
        sem = self._dsem(so)
        so.dcount += 16
        val = so.dcount
        self.dma_toks[sem] = val
        self.ninst += 1
        self.nwaits += len(waits)

        def run(e, waits=waits, sem=sem):
            for s, v in waits:
                e.wait_ge(s, v)
            e.dma_start(out=out_ap, in_=in_ap).then_inc(sem, 16)

        self.ops[eng].append(run)
        if disjoint:
            self._post((sem, val), in_ts, [])
            out_t.w[sem] = val
        else:
            self._post((sem, val), in_ts, [out_t])

    def group_done(self, owner, tiles):
        for t in tiles:
            t.w = {owner.dsem: owner.dcount}

    def cc_allgather(self, out_t, out_ap, in_t, in_ap, groups):
        waits = self._deps("gpsimd", [in_t], [out_t])
        sem = self._dsem(out_t)
        out_t.dcount += 1
        val = out_t.dcount
        self.ninst += 1

        def run(e, waits=waits, sem=sem):
            for s, v in waits:
                e.wait_ge(s, v)
            e.collective_compute("AllGather", ALU.bypass, replica_groups=groups,
                                 ins=[in_ap.opt()], outs=[out_ap.opt()]).then_inc(sem)

        self.ops["gpsimd"].append(run)
        self._post((sem, val), [in_t], [out_t])

    def drain(self, scratch):
        waits = []
        seen = self.seen["gpsimd"]
        for s, v in self.dma_toks.items():
            if seen.get(s, 0) < v:
                seen[s] = v
                waits.append((s, v))

        def run(e, waits=waits):
            for s, v in waits:
                e.wait_ge(s, v)

        self.ops["gpsimd"].append(run)
        self.op("gpsimd", lambda e: e.memset(scratch[:], 0.0), writes=[scratch])
        self.barrier()

    def wait_all(self, eng, tiles):
        waits = self._deps(eng, tiles, [])

        def run(e, waits=waits):
            for s, v in waits:
                e.wait_ge(s, v)

        self.ops[eng].append(run)

    def barrier(self):
        for e in self.ENGS:
            waits = []
            for o in ("tensor", "vector", "scalar", "gpsimd"):
                assert not self.pending[o], o
                v = self.count[o]
                s = self.sem[o]
                if o == e or v == 0:
                    continue
                if self.seen[e].get(s, 0) < v:
                    self.seen[e][s] = v
                    waits.append((s, v))

            def run(eng, waits=waits):
                for s, v in waits:
                    eng.wait_ge(s, v)

            self.ops[e].append(run)

    def finish(self):
        for e in self.ENGS:
            assert not self.pending[e], e
        ops = self.ops
        with self.nc.Block() as block:
            @block.tensor
            def _(e):
                for f in ops["tensor"]:
                    f(e)

            @block.vector
            def _(e):
                for f in ops["vector"]:
                    f(e)

            @block.scalar
            def _(e):
                for f in ops["scalar"]:
                    f(e)

            @block.gpsimd
            def _(e):
                for f in ops["gpsimd"]:
                    f(e)

            @block.sync
            def _(e):
                for f in ops["sync"]:
                    f(e)


def make_consts(P):
    if P._consts is not None:
        return P._consts
    c = {}
    P._consts = c
    c["ident"] = P.sb("ident", [128, 128], BF16)
    c["ones"] = P.sb("ones", [128, 128], BF16)
    c["tri"] = P.sb("tri", [128, 128], BF16)
    c["epsb"] = P.sb("epsb", [128, 1], F32)
    c["oneb"] = P.sb("oneb", [128, 1], F32)
    ident, ones, tri = c["ident"], c["ones"], c["tri"]
    P.op("vector", lambda e: e.memset(c["epsb"][:], EPS), writes=[c["epsb"]])
    P.op("vector", lambda e: e.memset(c["oneb"][:], 1.0), writes=[c["oneb"]])
    P.op("gpsimd", lambda e: e.memset(ident[:], 0.0), writes=[ident])
    P.op("gpsimd", lambda e: e.affine_select(out=ident[:], in_=ident[:], pattern=[[-1, 128]],
                                              compare_op=ALU.not_equal, fill=1.0, base=0,
                                              channel_multiplier=1), reads=[ident], writes=[ident])
    P.op("gpsimd", lambda e: e.memset(ones[:], 1.0), writes=[ones])
    P.op("gpsimd", lambda e: e.memset(tri[:], 1.0), writes=[tri])
    P.op("gpsimd", lambda e: e.affine_select(out=tri[:], in_=tri[:], pattern=[[-1, 128]],
                                              compare_op=ALU.is_ge, fill=0.0, base=0,
                                              channel_multiplier=1), reads=[tri], writes=[tri])
    return c


def make_masks(P, strict, name):
    ms = []
    for dj in range(4):
        m = P.sb("%s%d" % (name, dj), [128, 2, 512], BF16)
        P.op("gpsimd", lambda e, m=m: e.memset(m[:], 1.0), writes=[m])
        P.op("gpsimd", lambda e, m=m, dj=dj: e.affine_select(
            out=m[:], in_=m[:], pattern=[[0, 2], [1, 512]],
            compare_op=(ALU.is_gt if strict else ALU.is_ge), fill=0.0, base=-dj * 128,
            channel_multiplier=-1), reads=[m], writes=[m])
        ms.append(m)
    return ms


class NormTr:
    def __init__(self, P, c, gain_t, pT_list, tag, stack=None):
        self.P, self.c, self.g = P, c, gain_t
        self.pT = pT_list
        self.i = 0
        self.junk = P.sb(tag + "junk", [128, 1024], BF16, stack)
        self.ss = [P.sb(tag + "ss%d" % i, [128, 1], F32, stack) for i in range(2)]
        self.ln = [P.sb(tag + "ln%d" % i, [128, 1], F32, stack) for i in range(2)]
        self.rs = [P.sb(tag + "rs%d" % i, [128, 1], F32, stack) for i in range(2)]
        self.hn = [P.sb(tag + "hn%d" % i, [128, 1024], BF16, stack) for i in range(2)]

    def stats(self, xt, xap):
        P, c = self.P, self.c
        i = self.i
        ss, ln, rs = self.ss[i % 2], self.ln[i % 2], self.rs[i % 2]
        junk = self.junk
        P.op("scalar", lambda e: e.activation(out=junk[:], in_=xap, func=AF.Square, scale=1.0 / 32,
                                              accum_out=ss[:]), reads=[xt], writes=[junk, ss])
        P.op("scalar", lambda e: e.activation(out=ln[:], in_=ss[:], func=AF.Ln, bias=c["epsb"][:]),
             reads=[ss, c["epsb"]], writes=[ln])
        P.op("scalar", lambda e: e.activation(out=rs[:], in_=ln[:], func=AF.Exp, scale=-0.5),
             reads=[ln], writes=[rs])
        return rs

    def pre(self, xt, xap):
        P = self.P
        rs = self.stats(xt, xap)
        i = self.i
        self.i += 1
        hn = self.hn[i % 2]
        g = self.g
        P.op("vector", lambda e: e.scalar_tensor_tensor(out=hn[:], in0=xap, scalar=rs[:], in1=g[:],
                                                        op0=ALU.mult, op1=ALU.mult),
             reads=[xt, rs, g], writes=[hn])
        return (i, hn)

    def post(self, h, hT, col0, evac_eng="vector"):
        P, c = self.P, self.c
        i, hn = h
        pT = self.pT[i % len(self.pT)]
        for k in range(8):
            P.op("tensor", lambda e, k=k: e.transpose(out=pT[:, k, :], in_=hn[:, k * 128:(k + 1) * 128],
                                                      identity=c["ident"][:]),
                 reads=[hn, c["ident"]], writes=[pT], signal=(k == 7))
        if evac_eng == "vector":
            P.op("vector", lambda e: e.tensor_copy(out=hT[:, :, col0:col0 + 128], in_=pT[:]),
                 reads=[pT], writes=[hT])
        else:
            P.op("scalar", lambda e: e.copy(out=hT[:, :, col0:col0 + 128], in_=pT[:]),
                 reads=[pT], writes=[hT])

    def run(self, xt, xap, hT, col0, evac_eng="vector"):
        self.post(self.pre(xt, xap), hT, col0, evac_eng)


def load_gain_bcast(P, name, dram_t, ap1d):
    g = P.sb(name, [128, D], F32)
    P.dma("sync", g, g[:], dram_t, ap1d.partition_broadcast(128))
    return g


def build_m0(S):
    nc = bass.Bass("TRN2", target_bir_lowering=False)
    NCH = S // 512
    x = nc.dram_tensor("x", [S, D], F32, kind="ExternalInput").ap()
    gm = nc.dram_tensor("gm", [D], F32, kind="ExternalInput").ap()
    w = nc.dram_tensor("w", [D, 768], F32, kind="ExternalInput").ap()
    cw = nc.dram_tensor("cw", [128, 3], F32, kind="ExternalInput").ap()
    mixo = nc.dram_tensor("mixo", [256, S], BF16, kind="ExternalOutput").ap()
    with ExitStack() as st:
        P = Prog(nc, st)
        emit_m0(P, st, S, x, gm, w, cw, mixo)
        P.finish()
    return nc


def emit_m0(P, st, S, x, gm, w, cw, mixo, chunk_done=None):
    NCH = S // 512
    NT128 = S // 128
    xd, gmd, wd, cwd = P.dram(x, "x"), P.dram(gm, "gm"), P.dram(w, "w"), P.dram(cw, "cw")
    mixod = P.dram(mixo, "mixo") if mixo is not None else None
    c = make_consts(P)
    masks = make_masks(P, True, "msb") if not DEBUG.get("no_masks") else None
    QT = P.sb("QT", [128, S], BF16)
    KTs = P.sb("KTs", [128, S], BF16)
    KTn = P.sb("KTn", [128, S], BF16)
    Vt = P.sb("Vt", [128, NT128, 128], BF16)
    AOT = P.sb("AOT", [128, S], BF16)
    COT = P.sb("COT", [128, S], BF16)
    wt = P.sb("wt", [128, 8, 768], BF16)
    cwt = P.sb("cwt", [128, 3], F32)
    P.dma("gpsimd", wt, wt[:], wd, w.rearrange("(k p) n -> p k n", p=128))
    P.dma("sync", cwt, cwt[:], cwd, cw)
    g = load_gain_bcast(P, "gmix", gmd, gm)

    with ExitStack() as s1:
        xts = [P.sb("xt%d" % i, [128, D], F32, s1) for i in range(3)]
        hTs = [P.sb("hT%d" % i, [128, 8, 512], BF16, s1) for i in range(2)]
        pTs = [P.ps("pT%d" % i, [128, 8, 128], BF16, s1) for i in range(2)]
        pPs = [P.ps("pP%d" % i, [128, 512], F32, s1) for i in range(3)]
        pV = P.ps("pV", [128, 4, 128], F32, s1)
        Bs = [P.sb("Bs%d" % i, [128, 512], F32, s1) for i in range(2)]
        Cs = [P.sb("Cs%d" % i, [128, 512], F32, s1) for i in range(2)]
        cus = [P.sb("cu%d" % i, [128, 514], F32, s1) for i in range(2)]
        ys = [P.sb("y%d" % i, [128, 512], F32, s1) for i in range(2)]
        nt = NormTr(P, c, g, pTs, "n1", s1)
        P.op("gpsimd", lambda e: e.memset(cus[0][:, 0:2], 0.0), writes=[cus[0]])
        pi = 0
        for ch in range(NCH if not DEBUG.get("no_p1") else 0):
            hT = hTs[ch % 2]

            def load_pre(ti):
                xt = xts[ti % 3]
                P.dma("sync", xt, xt[:], xd, x[ti * 128:(ti + 1) * 128, :])
                return nt.pre(xt, xt[:])

            if ch == 0:
                nxt = load_pre(0)
            for tl in range(4):
                ti = ch * 4 + tl
                cur = nxt
                if ti + 1 < NCH * 4:
                    nxt = load_pre(ti + 1)
                nt.post(cur, hT, tl * 128, evac_eng=("vector" if tl % 2 == 0 else "scalar"))
            cs = slice(ch * 512, (ch + 1) * 512)

            def proj(j):
                nonlocal pi
                pp = pPs[pi % 3]
                pi += 1
                for k in range(8):
                    P.op("tensor", lambda e, k=k, pp=pp: e.matmul(pp[:], lhsT=wt[:, k, j * 128:(j + 1) * 128],
                                                                  rhs=hT[:, k, :], start=(k == 0), stop=(k == 7)),
                         reads=[wt, hT], writes=[pp], signal=(k == 7))
                return pp

            if DEBUG.get("no_proj"):
                continue
            pp = proj(0)
            P.op("scalar", lambda e, pp=pp: e.copy(out=QT[:, cs], in_=pp[:]), reads=[pp], writes=[QT])
            if DEBUG.get("only_q"):
                continue
            pp = proj(1)
            if not DEBUG.get("k_dve_only"):
                P.op("scalar", lambda e, pp=pp: e.activation(out=KTs[:, cs], in_=pp[:], func=AF.Copy, scale=0.125),
                     reads=[pp], writes=[KTs])
            if not DEBUG.get("k_act_only"):
              P.op("vector", lambda e, pp=pp: e.tensor_scalar(out=KTn[:, cs], in0=pp[:], scalar1=-0.125, scalar2=None,
                                                            op0=ALU.mult), reads=[pp], writes=[KTn])
            if DEBUG.get("only_qk"):
                continue
            for tl in range(4):
                for k in range(8):
                    P.op("tensor", lambda e, k=k, tl=tl: e.matmul(pV[:, tl, :], lhsT=hT[:, k, tl * 128:(tl + 1) * 128],
                                                                  rhs=wt[:, k, 256:384], start=(k == 0), stop=(k == 7)),
                         reads=[hT, wt], writes=[pV], signal=(k == 7 and tl == 3))
            P.op("vector", lambda e: e.tensor_copy(out=Vt[:, ch * 4:(ch + 1) * 4, :], in_=pV[:]), reads=[pV], writes=[Vt])
            if DEBUG.get("no_conv"):
                continue
            Bsb, Csb, cu, cun, y = Bs[ch % 2], Cs[ch % 2], cus[ch % 2], cus[(ch + 1) % 2], ys[ch % 2]
            pp = proj(3)
            P.op("scalar", lambda e, pp=pp: e.copy(out=Bsb[:], in_=pp[:]), reads=[pp], writes=[Bsb])
            pp = proj(4)
            P.op("scalar", lambda e, pp=pp: e.copy(out=Csb[:], in_=pp[:]), reads=[pp], writes=[Csb])
            pp = proj(5)
            P.op("vector", lambda e, pp=pp: e.tensor_tensor(out=cu[:, 2:514], in0=pp[:], in1=Csb[:], op=ALU.mult),
                 reads=[pp, Csb], writes=[cu])
            P.op("vector", lambda e: e.tensor_scalar(out=y[:], in0=cu[:, 0:512], scalar1=cwt[:, 0:1], scalar2=None,
                                                     op0=ALU.mult), reads=[cu, cwt], writes=[y])
            P.op("vector", lambda e: e.scalar_tensor_tensor(out=y[:], in0=cu[:, 1:513], scalar=cwt[:, 1:2], in1=y[:],
                                                            op0=ALU.mult, op1=ALU.add), reads=[cu, cwt, y], writes=[y])
            P.op("vector", lambda e: e.scalar_tensor_tensor(out=y[:], in0=cu[:, 2:514], scalar=cwt[:, 2:3], in1=y[:],
                                                            op0=ALU.mult, op1=ALU.add), reads=[cu, cwt, y], writes=[y])
            P.op("gpsimd", lambda e: e.tensor_tensor(out=COT[:, cs], in0=y[:], in1=Bsb[:], op=ALU.mult),
                 reads=[y, Bsb], writes=[COT])
            P.op("gpsimd", lambda e: e.tensor_copy(out=cun[:, 0:2], in_=cu[:, 512:514]), reads=[cu], writes=[cun])
        P.barrier()
    if mixo is not None:
        P.dma("sync", mixod, mixo[128:256, :], COT, COT[:], disjoint=True)

    with ExitStack() as s2:
        if DEBUG.get("skip_p2"):
            NCH = 0
        S2s = P.ps("S2s", [128, 2, 512], F32, s2)
        X2s = [P.ps("X2_%d" % i, [128, 2, 512], F32, s2) for i in range(2)]
        Oh = [P.ps("O%d" % i, [128, 512], F32, s2) for i in range(2)]
        esb = [P.sb("esb%d" % i, [128, 2, 512], F32, s2) for i in range(2)]
        spb = [P.sb("spb%d" % i, [128, 2, 512], BF16, s2) for i in range(2)]
        pbf = [P.sb("pbf%d" % i, [128, 2, 512], BF16, s2) for i in range(2)]
        Rf = P.sb("Rf", [128, 2, 512], F32, s2)
        Rb = [P.sb("Rb%d" % i, [128, 2, 512], BF16, s2) for i in range(2)]
        for t_ in spb + pbf:
            P.op("gpsimd", lambda e, t_=t_: e.memset(t_[:], 0.0), writes=[t_])
        def mk(qc):
            qs = slice(qc * 512, (qc + 1) * 512)
            nj = 4 * qc + 4

            def A(j):
                for h in range(2):
                    hs = slice(h * 64, (h + 1) * 64)
                    P.op("tensor", lambda e, h=h, hs=hs: e.matmul(
                        S2s[:, h, :], lhsT=KTs[hs, j * 128:(j + 1) * 128], rhs=QT[hs, qs], start=True, stop=True),
                        reads=[KTs, QT], writes=[S2s], signal=(h == 1))

            def B(j):
                es, sp = esb[j % 2], spb[j % 2]
                c0 = max(0, j - 4 * qc) * 128
                P.op("scalar", lambda e: e.activation(out=es[:, :, c0:], in_=S2s[:, :, c0:], func=AF.Exp),
                     reads=[S2s], writes=[es])
                P.op("scalar", lambda e: e.activation(out=sp[:, :, c0:], in_=es[:, :, c0:], func=AF.Ln, bias=c["oneb"][:]),
                     reads=[es, c["oneb"]], writes=[sp])
                if j >= 4 * qc:
                    m = masks[j - 4 * qc]
                    w_ = (j - 4 * qc + 1) * 128
                    P.op("vector", lambda e: e.tensor_tensor(out=sp[:, :, :w_], in0=sp[:, :, :w_], in1=m[:, :, :w_], op=ALU.mult),
                         reads=[sp, m], writes=[sp])

            def C(j):
                sp = spb[j % 2]
                rb = Rb[j % 2]
                X2 = X2s[j % 2]
                first = (j == nj - 1)
                for h in range(2):
                    hs = slice(h * 64, (h + 1) * 64)
                    P.op("tensor", lambda e, h=h: e.matmul(X2[:, h, :], lhsT=c["tri"][:], rhs=sp[:, h, :],
                                                           start=True, stop=False),
                         reads=[c["tri"], sp], writes=[X2], signal=False)
                    if not first:
                        P.op("tensor", lambda e, h=h: e.matmul(X2[:, h, :], lhsT=c["ones"][:], rhs=rb[:, h, :],
                                                               start=False, stop=False),
                             reads=[c["ones"], rb], writes=[X2], signal=False)
                    P.op("tensor", lambda e, h=h, hs=hs: e.matmul(X2[:, h, :], lhsT=KTn[hs, j * 128:(j + 1) * 128],
                                                                  rhs=QT[hs, qs], start=False, stop=True),
                         reads=[KTn, QT], writes=[X2], signal=(h == 1))

            def Dd(j):
                if j == 0:
                    return
                sp = spb[j % 2]
                rbn = Rb[(j - 1) % 2]
                if j == nj - 1:
                    P.op("vector", lambda e: e.tensor_copy(out=Rf[:], in_=sp[:]), reads=[sp], writes=[Rf])
                    P.op("vector", lambda e: e.tensor_copy(out=rbn[:], in_=sp[:]), reads=[sp], writes=[rbn])
                else:
                    P.op("vector", lambda e: e.tensor_tensor(out=Rf[:], in0=Rf[:], in1=sp[:], op=ALU.add),
                         reads=[Rf, sp], writes=[Rf])
                    P.op("vector", lambda e: e.tensor_copy(out=rbn[:], in_=Rf[:]), reads=[Rf], writes=[rbn])

            def E(j):
                pb = pbf[j % 2]
                X2 = X2s[j % 2]
                c0 = max(0, j - 4 * qc) * 128
                P.op("scalar", lambda e: e.activation(out=pb[:, :, c0:], in_=X2[:, :, c0:], func=AF.Exp, scale=-1.0),
                     reads=[X2], writes=[pb])
                if j >= 4 * qc:
                    m = masks[j - 4 * qc]
                    w_ = (j - 4 * qc + 1) * 128
                    P.op("vector", lambda e: e.tensor_tensor(out=pb[:, :, :w_], in0=pb[:, :, :w_], in1=m[:, :, :w_], op=ALU.mult),
                         reads=[pb, m], writes=[pb])

            def Ff(j):
                pb = pbf[j % 2]
                for h in range(2):
                    P.op("tensor", lambda e, h=h: e.matmul(Oh[h][:], lhsT=Vt[:, j, :], rhs=pb[:, h, :],
                                                           start=(j == nj - 1), stop=(j == 0)),
                         reads=[Vt, pb], writes=[Oh[h]], signal=True)

            def prologue():
                A(nj - 1)
                B(nj - 1)
                if nj >= 2:
                    A(nj - 2)

            return dict(A=A, B=B, C=C, Dd=Dd, E=E, Ff=Ff, qs=qs, nj=nj, prologue=prologue)

        blks = [mk(qc) for qc in range(NCH)]
        if blks:
            blks[0]["prologue"]()
        for qc in range(NCH):
            b = blks[qc]
            A, B, C, Dd, E, Ff, qs, nj = b["A"], b["B"], b["C"], b["Dd"], b["E"], b["Ff"], b["qs"], b["nj"]
            for j in range(nj - 1, -1, -1):
                C(j)
                if j + 1 <= nj - 1:
                    Ff(j + 1)
                Dd(j)
                if j - 1 >= 0:
                    B(j - 1)
                if j - 2 >= 0:
                    A(j - 2)
                E(j)
            if qc + 1 < NCH:
                blks[qc + 1]["prologue"]()
            Ff(0)
            for h in range(2):
                hs = slice(h * 64, (h + 1) * 64)
                if h == 0:
                    P.op("vector", lambda e, hs=hs, h=h: e.tensor_copy(out=AOT[hs, qs], in_=Oh[h][hs, :]),
                         reads=[Oh[h]], writes=[AOT])
                else:
                    P.op("scalar", lambda e, hs=hs, h=h: e.copy(out=AOT[hs, qs], in_=Oh[h][hs, :]),
                         reads=[Oh[h]], writes=[AOT])
            if chunk_done is not None:
                chunk_done(qc, qs, [AOT, COT])
        P.barrier()
    if mixo is not None:
        if not DEBUG.get("skip_p2"):
            P.dma("sync", mixod, mixo[0:128, :], AOT, AOT[:], disjoint=True)
        P.wait_all("sync", [mixod])


def build_dense(NT, final):
    nc = bass.Bass("TRN2", target_bir_lowering=False)
    mixT = nc.dram_tensor("mixT", [D, NT], BF16, kind="ExternalInput").ap()
    x = nc.dram_tensor("x", [NT, D], F32, kind="ExternalInput").ap()
    wo = nc.dram_tensor("wo", [D, D], F32, kind="ExternalInput").ap()
    g2 = nc.dram_tensor("g2", [D], F32, kind="ExternalInput").ap()
    wu = nc.dram_tensor("wu", [D, DFF], F32, kind="ExternalInput").ap()
    wdn = nc.dram_tensor("wdn", [DFF, D], F32, kind="ExternalInput").ap()
    g3 = nc.dram_tensor("g3", [D], F32, kind="ExternalInput").ap()
    if final:
        y = nc.dram_tensor("y", [NT, D], F32, kind="ExternalOutput").ap()
        hTo = None
    else:
        y = nc.dram_tensor("xo", [NT, D], F32, kind="ExternalOutput").ap()
        hTo = nc.dram_tensor("hTo", [D, NT], BF16, kind="ExternalOutput").ap()
    with ExitStack() as st:
        P = Prog(nc, st)
        emit_dense(P, st, NT, final, mixT, x, wo, g2, wu, wdn, g3, y, hTo)
        P.finish()
    return nc


def emit_dense(P, st, NT, final, mixT, x, wo, g2, wu, wdn, g3, y, hTo, load_mixT=None, ht_done=None):
    NTL = NT // 128
    NCH = NT // 512
    mixTd, xd, wod, g2d, wud, wdnd, g3d, yd = (P.dram(mixT, "mixT") if mixT is not None else None, x if isinstance(x, T) else P.dram(x, "x"), P.dram(wo, "wo"),
                                                P.dram(g2, "g2"), P.dram(wu, "wu"), P.dram(wdn, "wdn"),
                                                P.dram(g3, "g3"), y if isinstance(y, T) else P.dram(y, "y"))
    if isinstance(x, T):
        x = x.t
    if isinstance(y, T):
        y = y.t
    hTod = P.dram(hTo, "hTo") if hTo is not None else None
    c = make_consts(P)
    gg2 = load_gain_bcast(P, "gg2", g2d, g2)
    gg3 = load_gain_bcast(P, "gg3", g3d, g3)
    X = [P.sb("X%d" % i, [128, D], F32) for i in range(NTL)]
    h2T = P.sb("h2T", [128, 8, NT], BF16)
    for g0 in range(0, NTL, 4):
        grp = X[g0:g0 + 4]
        for i in range(g0, min(g0 + 4, NTL)):
            P.dma("sync", X[i], X[i][:], xd, x[i * 128:(i + 1) * 128, :], owner=grp[0])
        P.group_done(grp[0], grp)
    pTs = [P.ps("pT%d" % i, [128, 8, 128], BF16) for i in range(2)]
    pA = [P.ps("pA%d" % i, [128, 512], F32) for i in range(4)]
    wuq0 = P.sb("wuq0", [128, 8, 1024], BF16)
    wdq0 = P.sb("wdq0", [128, 8, D], BF16)
    P.dma("gpsimd", wuq0, wuq0[:], wud, wu[:, 0:1024].rearrange("(k p) n -> p k n", p=128))
    P.dma("gpsimd", wdq0, wdq0[:], wdnd, wdn[0:1024, :].rearrange("(k p) n -> p k n", p=128))
    with ExitStack() as s1:
        mt = P.sb("mixTs", [128, 8, NT], BF16, s1)
        wot = P.sb("wot", [128, 8, D], BF16, s1)
        if load_mixT is not None:
            load_mixT(mt, s1)
        else:
            P.dma("sync", mt, mt[:], mixTd, mixT.rearrange("(k p) n -> p k n", p=128))
        P.dma("gpsimd", wot, wot[:], wod, wo.rearrange("(k p) n -> p k n", p=128))
        nt = NormTr(P, c, gg2, pTs, "n2", s1)
        hprev = None
        for i in range(NTL):
            for hf in range(2):
                pp = pA[(i * 2 + hf) % 4]
                for k in range(8):
                    P.op("tensor", lambda e, k=k, pp=pp, hf=hf, i=i: e.matmul(
                        pp[:], lhsT=mt[:, k, i * 128:(i + 1) * 128], rhs=wot[:, k, hf * 512:(hf + 1) * 512],
                        start=(k == 0), stop=(k == 7)), reads=[mt, wot], writes=[pp], signal=(k == 7))
                P.op("vector", lambda e, pp=pp, hf=hf, i=i: e.tensor_tensor(
                    out=X[i][:, hf * 512:(hf + 1) * 512], in0=pp[:], in1=X[i][:, hf * 512:(hf + 1) * 512], op=ALU.add),
                    reads=[pp, X[i]], writes=[X[i]])
            hcur = nt.pre(X[i], X[i][:])
            if hprev is not None:
                nt.post(hprev[0], h2T, hprev[1] * 128, evac_eng=("vector" if hprev[1] % 2 == 0 else "scalar"))
            hprev = (hcur, i)
        nt.post(hprev[0], h2T, hprev[1] * 128, evac_eng=("vector" if hprev[1] % 2 == 0 else "scalar"))
        P.barrier()
    with ExitStack() as s2:
        wuq = [wuq0, P.sb("wuq1", [128, 8, 1024], BF16, s2)]
        wdq = [wdq0, P.sb("wdq1", [128, 8, D], BF16, s2)]
        aT = [P.sb("aT%d" % i, [128, 8, 512], BF16, s2) for i in range(2)]
        rl = [P.sb("rl%d" % i, [128, 512], F32, s2) for i in range(2)]
        it = 0
        ri = 0
        for q in range(4):
            wuqt, wdqt = wuq[q % 2], wdq[q % 2]
            if q > 0:
                P.dma("gpsimd", wuqt, wuqt[:], wud, wu[:, q * 1024:(q + 1) * 1024].rearrange("(k p) n -> p k n", p=128))
                P.dma("gpsimd", wdqt, wdqt[:], wdnd, wdn[q * 1024:(q + 1) * 1024, :].rearrange("(k p) n -> p k n", p=128))
            for ch in range(NCH):
                a = aT[it % 2]
                it += 1
                ts = slice(ch * 512, (ch + 1) * 512)
                for m in range(8):
                    pp = pA[m % 2]
                    for k in range(8):
                        P.op("tensor", lambda e, k=k, m=m, pp=pp: e.matmul(
                            pp[:], lhsT=wuqt[:, k, m * 128:(m + 1) * 128], rhs=h2T[:, k, ts],
                            start=(k == 0), stop=(k == 7)), reads=[wuqt, h2T], writes=[pp], signal=(k == 7))
                    r = rl[ri % 2]
                    ri += 1
                    P.op("scalar", lambda e, pp=pp, r=r: e.activation(out=r[:], in_=pp[:], func=AF.Relu),
                         reads=[pp], writes=[r])
                    P.op("vector", lambda e, r=r, m=m, a=a: e.tensor_tensor(out=a[:, m, :], in0=r[:], in1=r[:], op=ALU.mult),
                         reads=[r], writes=[a])
                for tl in range(4):
                    i = ch * 4 + tl
                    for hf in range(2):
                        pp = pA[2 + (tl * 2 + hf) % 2]
                        for m in range(8):
                            P.op("tensor", lambda e, m=m, pp=pp, hf=hf, tl=tl, a=a: e.matmul(
                                pp[:], lhsT=a[:, m, tl * 128:(tl + 1) * 128], rhs=wdqt[:, m, hf * 512:(hf + 1) * 512],
                                start=(m == 0), stop=(m == 7)), reads=[a, wdqt], writes=[pp], signal=(m == 7))
                        P.op("vector", lambda e, pp=pp, hf=hf, i=i: e.tensor_tensor(
                            out=X[i][:, hf * 512:(hf + 1) * 512], in0=pp[:], in1=X[i][:, hf * 512:(hf + 1) * 512],
                            op=ALU.add), reads=[pp, X[i]], writes=[X[i]])
        P.barrier()
    if final:
        nt3 = NormTr(P, c, gg3, pTs, "n3")
        outs = [P.sb("yo%d" % i, [128, D], F32) for i in range(2)]
        for i in range(NTL):
            rs = nt3.stats(X[i], X[i][:])
            nt3.i += 1
            o = outs[i % 2]
            P.op("vector", lambda e, i=i, o=o, rs=rs: e.scalar_tensor_tensor(
                out=o[:], in0=X[i][:], scalar=rs[:], in1=gg3[:], op0=ALU.mult, op1=ALU.mult),
                reads=[X[i], rs, gg3], writes=[o])
            P.dma("sync", yd, y[i * 128:(i + 1) * 128, :], o, o[:], disjoint=True)
        P.wait_all("sync", [yd])
    else:
        nt3 = NormTr(P, c, gg3, pTs, "n3")
        hT3 = P.sb("hT3", [128, 8, NT], BF16)
        hprev = None
        for i in range(NTL):
            P.dma("sync", yd, y[i * 128:(i + 1) * 128, :], X[i], X[i][:], disjoint=True)
            hcur = nt3.pre(X[i], X[i][:])
            if hprev is not None:
                nt3.post(hprev[0], hT3, hprev[1] * 128, evac_eng=("vector" if hprev[1] % 2 == 0 else "scalar"))
            hprev = (hcur, i)
        nt3.post(hprev[0], hT3, hprev[1] * 128, evac_eng=("vector" if hprev[1] % 2 == 0 else "scalar"))
        if ht_done is not None:
            ht_done(hT3)
        else:
            P.dma("sync", hTod, hTo.rearrange("(k p) n -> p k n", p=128), hT3, hT3[:])
            P.wait_all("sync", [yd, hTod])


def build_m1(S, lam_init):
    nc = bass.Bass("TRN2", target_bir_lowering=False)
    hT = nc.dram_tensor("hT", [D, S], BF16, kind="ExternalInput").ap()
    w = nc.dram_tensor("w", [D, 768], F32, kind="ExternalInput").ap()
    lamv = nc.dram_tensor("lamv", [4, 64], F32, kind="ExternalInput").ap()
    sg = nc.dram_tensor("sg", [128, 1], F32, kind="ExternalInput").ap()
    mixo = nc.dram_tensor("mixo", [256, S], BF16, kind="ExternalOutput").ap()
    with ExitStack() as st:
        P = Prog(nc, st)
        emit_m1(P, st, S, lam_init, hT, w, lamv, sg, mixo)
        P.finish()
    return nc


def emit_m1(P, st, S, lam_init, hT, w, lamv, sg, mixo, ht_src=None, ch_order=None, chunk_done=None):
    NCH = S // 512
    NT128 = S // 128
    hTd = P.dram(hT, "hT") if hT is not None else None
    mixod = P.dram(mixo, "mixo") if mixo is not None else None
    wd, lamd, sgd = P.dram(w, "w"), P.dram(lamv, "lamv"), P.dram(sg, "sg")
    c = make_consts(P)
    masks = make_masks(P, False, "mdf")
    QT = [P.sb("QT%d" % h, [128, S], BF16) for h in range(2)]
    KT = [P.sb("KT%d" % h, [128, S], BF16) for h in range(2)]
    Vt = P.sb("Vt", [128, NT128, 256], BF16)
    AOT = [P.sb("AOT%d" % h, [128, S], BF16) for h in range(2)]
    wt = P.sb("wt", [128, 8, 768], BF16)
    P.dma("gpsimd", wt, wt[:], wd, w.rearrange("(k p) n -> p k n", p=128))
    lt = P.sb("lt", [128, 4, 64], F32)
    P.dma("sync", lt, lt[:], lamd, lamv.partition_broadcast(128))
    lp = P.sb("lp", [128, 2, 64], F32)
    ls = P.sb("ls", [128, 2], F32)
    le = P.sb("le", [128, 2], F32)
    nlam = P.sb("nlam", [128, 1], F32)
    sgt = P.sb("sgt", [128, 1], F32)
    sgs = P.sb("sgs", [128, 1], F32)
    P.dma("sync", sgt, sgt[:], sgd, sg)
    P.op("vector", lambda e: e.tensor_tensor(out=lp[:, 0, :], in0=lt[:, 0, :], in1=lt[:, 1, :], op=ALU.mult),
         reads=[lt], writes=[lp])
    P.op("vector", lambda e: e.tensor_tensor(out=lp[:, 1, :], in0=lt[:, 2, :], in1=lt[:, 3, :], op=ALU.mult),
         reads=[lt, lp], writes=[lp])
    P.op("vector", lambda e: e.reduce_sum(out=ls[:], in_=lp[:], axis=mybir.AxisListType.X), reads=[lp], writes=[ls])
    P.op("scalar", lambda e: e.activation(out=le[:], in_=ls[:], func=AF.Exp), reads=[ls], writes=[le])
    P.op("vector", lambda e: e.scalar_tensor_tensor(out=nlam[:], in0=le[:, 1:2], scalar=-lam_init, in1=le[:, 0:1],
                                                    op0=ALU.add, op1=ALU.subtract), reads=[le], writes=[nlam])
    P.op("vector", lambda e: e.tensor_scalar(out=sgs[:], in0=sgt[:], scalar1=1.0 - lam_init, scalar2=None, op0=ALU.mult),
         reads=[sgt], writes=[sgs])

    with ExitStack() as s1:
        hts = [P.sb("hTc%d" % i, [128, 8, 512], BF16, s1) for i in range(2)]
        pPs = [P.ps("pP%d" % i, [128, 512], F32, s1) for i in range(4)]
        pV = P.ps("pV", [128, 4, 256], F32, s1)
        pi = 0
        for ci, ch in enumerate(ch_order if ch_order is not None else range(NCH)):
            ht = hts[ci % 2]
            cs = slice(ch * 512, (ch + 1) * 512)
            if ht_src is not None:
                src_t, src_ap = ht_src(ch)
                P.dma("sync", ht, ht[:], src_t, src_ap)
            else:
                P.dma("sync", ht, ht[:], hTd, hT[:, cs].rearrange("(k p) n -> p k n", p=128))
            for j in range(4):
                pp = pPs[pi % 4]
                pi += 1
                for k in range(8):
                    P.op("tensor", lambda e, k=k, pp=pp, j=j: e.matmul(pp[:], lhsT=wt[:, k, j * 128:(j + 1) * 128],
                                                                       rhs=ht[:, k, :], start=(k == 0), stop=(k == 7)),
                         reads=[wt, ht], writes=[pp], signal=(k == 7))
                dst = (QT[0], QT[1], KT[0], KT[1])[j]
                if j % 2 == 0:
                    P.op("scalar", lambda e, pp=pp, dst=dst: e.copy(out=dst[:, cs], in_=pp[:]), reads=[pp], writes=[dst])
                else:
                    P.op("vector", lambda e, pp=pp, dst=dst: e.tensor_copy(out=dst[:, cs], in_=pp[:]), reads=[pp], writes=[dst])
            for tl in range(4):
                for k in range(8):
                    P.op("tensor", lambda e, k=k, tl=tl: e.matmul(pV[:, tl, :], lhsT=ht[:, k, tl * 128:(tl + 1) * 128],
                                                                  rhs=wt[:, k, 512:768], start=(k == 0), stop=(k == 7)),
                         reads=[ht, wt], writes=[pV], signal=(k == 7 and tl == 3))
            P.op("vector", lambda e: e.tensor_copy(out=Vt[:, ch * 4:(ch + 1) * 4, :], in_=pV[:]), reads=[pV], writes=[Vt])
        P.barrier()

    with ExitStack() as s2:
        S2 = [P.ps("S2_%d" % i, [128, 2, 512], F32, s2) for i in range(2)]
        O2 = P.ps("O2", [128, 2, 512], F32, s2)
        L2t = P.ps("L2", [128, 2, 512], F32, s2)
        L2a = T(L2t[:, 0, :], "L2a")
        L2b = T(L2t[:, 1, :], "L2b")
        L2a.psum = L2b.psum = True
        lnl = P.sb("lnl", [128, 2, 512], F32, s2)
        pbf = [P.sb("pbf%d" % i, [128, 2, 512], BF16, s2) for i in range(3)]
        rl = P.sb("rl", [128, 2, 512], F32, s2)
        acc = P.sb("acc", [128, 512], F32, s2)
        ahi = P.sb("ahi", [128, 512], BF16, s2)
        alo = P.sb("alo", [128, 512], BF16, s2)
        on = P.sb("on", [128, 2, 512], F32, s2)
        od = P.sb("od", [128, 512], F32, s2)
        sq = P.sb("sq", [128, 512], BF16, s2)
        lnv = P.sb("lnv", [128, 512], F32, s2)
        rsv = P.sb("rsv", [128, 512], F32, s2)
        for t_ in pbf:
            P.op("gpsimd", lambda e, t_=t_: e.memset(t_[:], 0.0), writes=[t_])
        pending = [None]
        first_a_done = [False]
        for qc in range(NCH):
            qs = slice(qc * 512, (qc + 1) * 512)
            nj = 4 * qc + 4
            for h in range(2):
                def A_blk(qc_, h_, j):
                    sb_ = S2[j % 2]
                    qs_ = slice(qc_ * 512, (qc_ + 1) * 512)
                    for m in range(2):
                        ms_ = slice(m * 64, (m + 1) * 64)
                        P.op("tensor", lambda e, m=m, ms_=ms_, sb_=sb_: e.matmul(
                            sb_[:, m, :], lhsT=KT[h_][ms_, j * 128:(j + 1) * 128], rhs=QT[h_][ms_, qs_],
                            start=True, stop=True), reads=[KT[h_], QT[h_]], writes=[sb_], signal=(m == 1))

                def A(j):
                    A_blk(qc, h, j)

                nxt_blk = (qc, 1) if h == 0 else ((qc + 1, 0) if qc + 1 < NCH else None)

                def E(j):
                    sb_, pb = S2[j % 2], pbf[j % 3]
                    c0 = max(0, j - 4 * qc) * 128
                    P.op("scalar", lambda e: e.activation(out=pb[:, :, c0:], in_=sb_[:, :, c0:], func=AF.Exp, scale=0.125),
                         reads=[sb_], writes=[pb])
                    if j >= 4 * qc:
                        m = masks[j - 4 * qc]
                        w_ = (j - 4 * qc + 1) * 128
                        P.op("vector", lambda e: e.tensor_tensor(out=pb[:, :, :w_], in0=pb[:, :, :w_], in1=m[:, :, :w_], op=ALU.mult),
                             reads=[pb, m], writes=[pb])

                def Ff(j):
                    pb = pbf[j % 3]
                    st_, sp_ = (j == nj - 1), (j == 0)
                    for m in range(2):
                        P.op("tensor", lambda e, m=m: e.matmul(O2[:, m, :], lhsT=Vt[:, j, h * 128:(h + 1) * 128],
                                                               rhs=pb[:, m, :], start=st_, stop=sp_),
                             reads=[Vt, pb], writes=[O2], signal=(m == 1))
                    P.op("tensor", lambda e: e.matmul(L2a[:], lhsT=c["ones"][:], rhs=pb[:, 0, :], start=st_, stop=sp_),
                         reads=[c["ones"], pb], writes=[L2a])
                    if st_:
                        P.op("vector", lambda e: e.tensor_copy(out=acc[:], in_=pb[:, 1, :]), reads=[pb], writes=[acc])
                    else:
                        P.op("vector", lambda e: e.tensor_tensor(out=acc[:], in0=acc[:], in1=pb[:, 1, :], op=ALU.add),
                             reads=[acc, pb], writes=[acc])

                def epi1():
                    P.op("vector", lambda e: e.tensor_copy(out=ahi[:], in_=acc[:]), reads=[acc], writes=[ahi])
                    P.op("vector", lambda e: e.tensor_tensor(out=alo[:], in0=acc[:], in1=ahi[:], op=ALU.subtract),
                         reads=[acc, ahi], writes=[alo])
                    P.op("tensor", lambda e: e.matmul(L2b[:], lhsT=c["ones"][:], rhs=ahi[:], start=True, stop=False),
                         reads=[c["ones"], ahi], writes=[L2b], signal=False)
                    P.op("tensor", lambda e: e.matmul(L2b[:], lhsT=c["ones"][:], rhs=alo[:], start=False, stop=True),
                         reads=[c["ones"], alo], writes=[L2b])

                def epi1b():
                    P.op("scalar", lambda e: e.activation(out=lnl[:], in_=L2t[:], func=AF.Ln), reads=[L2a, L2b], writes=[lnl])
                    P.op("scalar", lambda e: e.activation(out=rl[:], in_=lnl[:], func=AF.Exp, scale=-1.0), reads=[lnl], writes=[rl])
                    P.op("vector", lambda e: e.tensor_tensor(out=on[:], in0=O2[:], in1=rl[:], op=ALU.mult),
                         reads=[O2, rl], writes=[on])

                def epi2(h=h, qs=qs, qc=qc):
                    P.op("vector", lambda e: e.scalar_tensor_tensor(out=od[:], in0=on[:, 1, :], scalar=nlam[:], in1=on[:, 0, :],
                                                                    op0=ALU.mult, op1=ALU.add), reads=[on, nlam], writes=[od])
                    P.op("gpsimd", lambda e: e.tensor_tensor(out=sq[:], in0=od[:], in1=od[:], op=ALU.mult), reads=[od], writes=[sq])
                    P.op("tensor", lambda e: e.matmul(L2b[:], lhsT=c["ones"][:], rhs=sq[:], start=True, stop=True),
                         reads=[c["ones"], sq], writes=[L2b])
                    P.op("scalar", lambda e: e.activation(out=lnv[:], in_=L2b[:], func=AF.Ln, scale=1.0 / 128,
                                                          bias=c["epsb"][:]), reads=[L2b, c["epsb"]], writes=[lnv])
                    P.op("scalar", lambda e: e.activation(out=rsv[:], in_=lnv[:], func=AF.Exp, scale=-0.5), reads=[lnv], writes=[rsv])
                    P.op("vector", lambda e: e.scalar_tensor_tensor(out=AOT[h][:, qs], in0=od[:], scalar=sgs[:], in1=rsv[:],
                                                                    op0=ALU.mult, op1=ALU.mult),
                         reads=[od, sgs, rsv], writes=[AOT[h]])
                    if h == 1 and chunk_done is not None:
                        chunk_done(qc, qs, AOT)

                if not first_a_done[0]:
                    A(nj - 1)
                A(nj - 2)
                if pending[0] is not None:
                    pending[0][0]()
                E(nj - 1)
                E(nj - 2)
                A(nj - 3)
                pre_a, pre_e = {nj - 1, nj - 2, nj - 3}, {nj - 1, nj - 2}
                if pending[0] is not None:
                    pending[0][1]()
                for j in range(nj - 1, -1, -1):
                    if j not in pre_e:
                        E(j)
                    if j - 2 >= 0 and (j - 2) not in pre_a:
                        A(j - 2)
                    if j == 0:
                        first_a_done[0] = nxt_blk is not None
                        if nxt_blk is not None:
                            A_blk(nxt_blk[0], nxt_blk[1], 4 * nxt_blk[0] + 3)
                    Ff(j)
                    if j == nj - 4 and pending[0] is not None:
                        pending[0][2]()
                        pending[0] = None
                pending[0] = (epi1, epi1b, epi2)
        pending[0][0]()
        pending[0][1]()
        pending[0][2]()
        P.barrier()
    if mixo is not None:
        for h in range(2):
            P.dma("sync", mixod, mixo[h * 128:(h + 1) * 128, :], AOT[h], AOT[h][:], disjoint=True)
        P.wait_all("sync", [mixod])


GROUPS = [[0, 1, 2, 3], [4, 5, 6, 7]]


def build_fused(S, lam_init):
    nc = bass.Bass("TRN2", target_bir_lowering=False)
    NT, NCH = S // 4, S // 512
    CPR = NT // 512

    def inp(n, shp, d=F32):
        return nc.dram_tensor(n, shp, d, kind="ExternalInput").ap()

    x_all, x_own = inp("x_all", [S, D]), inp("x_own", [NT, D])
    gm0, w0, cw, sel = inp("gm0", [D]), inp("w0", [D, 768]), inp("cw", [128, 3]), inp("sel", [128, 4])
    wo0, g20, wu0, wdn0, gm1 = inp("wo0", [D, D]), inp("g20", [D]), inp("wu0", [D, DFF]), inp("wdn0", [DFF, D]), inp("gm1", [D])
    w1, lamv, sg = inp("w1", [D, 768]), inp("lamv", [4, 64]), inp("sg", [128, 1])
    wo1, g21, wu1, wdn1, gfin = inp("wo1", [D, D]), inp("g21", [D]), inp("wu1", [D, DFF]), inp("wdn1", [DFF, D]), inp("gfin", [D])
    y = nc.dram_tensor("y", [NT, D], F32, kind="ExternalOutput").ap()
    src1 = nc.dram_tensor("src1", [NCH, 256, 512], BF16).ap()
    dst1 = nc.dram_tensor("dst1", [NCH, 1024, 512], BF16).ap()
    src3 = nc.dram_tensor("src3", [NCH, 256, 512], BF16).ap()
    dst3 = nc.dram_tensor("dst3", [NCH, 1024, 512], BF16).ap()
    src2 = nc.dram_tensor("src2", [CPR, 1024, 512], BF16).ap()
    dst2 = nc.dram_tensor("dst2", [CPR, 4096, 512], BF16).ap()
    xres = nc.dram_tensor("xres", [NT, D], F32).ap()
    with ExitStack() as st:
        P = Prog(nc, st)
        make_consts(P)
        scratch = P.sb("scr", [128, 8], F32)
        selt = P.sb("selt", [128, 4], F32)
        P.dma("sync", selt, selt[:], P.dram(sel, "sel"), sel)
        xresT = P.dram(xres, "xres")
        src1T = [P.dram(src1, "src1")] * NCH
        dst1T = [P.dram(dst1, "dst1")] * NCH
        src3T = [P.dram(src3, "src3")] * NCH
        dst3T = [P.dram(dst3, "dst3")] * NCH
        src2T = [P.dram(src2[i], "src2_%d" % i) for i in range(CPR)]
        dst2T = [P.dram(dst2[i], "dst2_%d" % i) for i in range(CPR)]

        def stage(prefix, fn):
            with ExitStack() as ss:
                old = P.stack
                P.stack = ss
                P.prefix = prefix
                fn(ss)
                P.drain(scratch)
                P.stack = old

        def m0_chunk_done(qc, qs, tiles):
            AOT, COT = tiles
            P.dma("sync", src1T[qc], src1[qc, 0:128, :], AOT, AOT[:, qs])
            P.dma("sync", src1T[qc], src1[qc, 128:256, :], COT, COT[:, qs])
            P.cc_allgather(dst1T[qc], dst1[qc], src1T[qc], src1[qc], GROUPS)

        def m1_chunk_done(qc, qs, AOT):
            for h in range(2):
                P.dma("sync", src3T[qc], src3[qc, h * 128:(h + 1) * 128, :], AOT[h], AOT[h][:, qs])
            P.cc_allgather(dst3T[qc], dst3[qc], src3T[qc], src3[qc], GROUPS)

        def make_load_mixT(dstT, dst):
            def load(mt, sstack):
                cands = [P.sb("cand%d" % i, [128, 2, NT], BF16, sstack) for i in range(2)]
                ci = 0
                for g in range(4):
                    for r in range(4):
                        cd = cands[ci % 2]
                        ci += 1
                        for h in range(2):
                            P.dma("sync", cd, cd[:, h, :].rearrange("p (q c) -> p q c", q=CPR), [dstT[0]],
                                  dst[r * CPR:(r + 1) * CPR, g * 256 + h * 128:g * 256 + (h + 1) * 128, :].rearrange("q p c -> p q c"),
                                  disjoint=(h == 1))
                        mo = mt[:, 2 * g:2 * g + 2, :]
                        if r == 0:
                            P.op("vector", lambda e: e.tensor_scalar(out=mo, in0=cd[:], scalar1=selt[:, 0:1], scalar2=None,
                                                                     op0=ALU.mult), reads=[cd, selt], writes=[mt])
                        else:
                            P.op("vector", lambda e: e.scalar_tensor_tensor(out=mo, in0=cd[:], scalar=selt[:, r:r + 1], in1=mo,
                                                                            op0=ALU.mult, op1=ALU.add),
                                 reads=[cd, selt, mt], writes=[mt])
            return load

        def d0_ht_done(hT3):
            for i in range(CPR):
                P.dma("sync", src2T[i], src2[i].rearrange("(k p) n -> p k n", p=128), hT3, hT3[:, :, i * 512:(i + 1) * 512])
                P.cc_allgather(dst2T[i], dst2[i], src2T[i], src2[i], GROUPS)

        def m1_ht_src(ch):
            r, i = ch // CPR, ch % CPR
            return dst2T[i], dst2[i, r * 1024:(r + 1) * 1024, :].rearrange("(k p) n -> p k n", p=128)

        stage("a_", lambda ss: emit_m0(P, ss, S, x_all, gm0, w0, cw, None, chunk_done=m0_chunk_done))
        stage("b_", lambda ss: emit_dense(P, ss, NT, False, None, x_own, wo0, g20, wu0, wdn0, gm1, xresT, None,
                                          load_mixT=make_load_mixT(dst1T, dst1), ht_done=d0_ht_done))
        stage("c_", lambda ss: emit_m1(P, ss, S, lam_init, None, w1, lamv, sg, None, ht_src=m1_ht_src,
                                       ch_order=[r * CPR + i for i in range(CPR) for r in range(4)],
                                       chunk_done=m1_chunk_done))
        stage("d_", lambda ss: emit_dense(P, ss, NT, True, None, xresT, wo1, g21, wu1, wdn1, gfin, y, None,
                                          load_mixT=make_load_mixT(dst3T, dst3)))
        P.finish()
        print("fused program: ninst", P.ninst, "nwaits", P.nwaits)
    return nc


def fused_inputs(x, norm_mix, norm_mlp, norm_final, w_in_even, conv_w, w_out_even, w_in_odd,
                 lamv, subln_g, w_out_odd, w_up, w_down):
    B, S, _ = x.shape
    NT = S // 4
    perm = np.concatenate([np.concatenate([np.arange(g * 128, (g + 1) * 128), 512 + np.arange(g * 128, (g + 1) * 128)])
                           for g in range(4)])
    wo0p = np.ascontiguousarray(w_out_even[0][perm])
    maps = []
    for c in range(B * 4):
        b, g = c // 4, c % 4
        cols0 = np.concatenate([np.arange(sec * 512 + g * 128, sec * 512 + (g + 1) * 128) for sec in range(6)])
        cols1 = np.concatenate([np.arange(sec * 1024 + g * 256, sec * 1024 + (g + 1) * 256) for sec in range(3)])
        sel = np.zeros((128, 4), np.float32)
        sel[:, g] = 1.0
        maps.append({
            "x_all": x[b], "x_own": np.ascontiguousarray(x[b, g * NT:(g + 1) * NT]),
            "gm0": norm_mix[0], "w0": np.ascontiguousarray(w_in_even[0][:, cols0]),
            "cw": np.ascontiguousarray(conv_w[0][:, g * 128:(g + 1) * 128].T), "sel": sel,
            "wo0": wo0p, "g20": norm_mlp[0], "wu0": w_up[0], "wdn0": w_down[0], "gm1": norm_mix[1],
            "w1": np.ascontiguousarray(w_in_odd[0][:, cols1]), "lamv": lamv,
            "sg": np.ascontiguousarray(subln_g[0].reshape(128, 1)),
            "wo1": w_out_odd[0], "g21": norm_mlp[1], "wu1": w_up[1], "wdn1": w_down[1], "gfin": norm_final,
        })
    return maps


_CACHE = {}


def _prog(key, fn):
    if key not in _CACHE:
        _CACHE[key] = fn()
    return _CACHE[key]


def _run(nc, in_maps):
    res = run_bass_kernel_spmd(nc, in_maps, core_ids=list(range(len(in_maps))))
    return res.results


def kernel(x, norm_mix, norm_mlp, norm_final, w_in_even, conv_w, w_out_even, w_in_odd,
           lam_q1, lam_k1, lam_q2, lam_k2, subln_g, w_out_odd, w_up, w_down):
    f32 = lambda a: np.ascontiguousarray(np.asarray(a, dtype=np.float32))
    x = f32(x)
    B, S, _ = x.shape
    NT = S // 4
    nco = B * 4
    norm_mix, norm_mlp, norm_final = f32(norm_mix), f32(norm_mlp), f32(norm_final)
    w_in_even, conv_w, w_out_even, w_in_odd = f32(w_in_even), f32(conv_w), f32(w_out_even), f32(w_in_odd)
    w_out_odd, w_up, w_down, subln_g = f32(w_out_odd), f32(w_up), f32(w_down), f32(subln_g)
    lamv = np.ascontiguousarray(np.stack([f32(lam_q1)[0], f32(lam_k1)[0], f32(lam_q2)[0], f32(lam_k2)[0]], axis=0))
    lam_init = 0.8 - 0.6 * math.exp(-0.3 * 1)

    if not DEBUG.get("unfused"):
        assert B == 2
        maps = fused_inputs(x, norm_mix, norm_mlp, norm_final, w_in_even, conv_w, w_out_even, w_in_odd,
                            lamv, subln_g, w_out_odd, w_up, w_down)
        res = _run(_prog(("fused", S), lambda: build_fused(S, lam_init)), maps)
        out = np.stack([np.concatenate([np.asarray(res[b * 4 + r]["y"]) for r in range(4)], axis=0) for b in range(B)], axis=0)
        return out.astype(np.float32)

    w0 = w_in_even[0]
    maps = []
    for c in range(nco):
        b, g = c // 4, c % 4
        cols = np.concatenate([np.arange(sec * 512 + g * 128, sec * 512 + (g + 1) * 128) for sec in range(6)])
        maps.append({"x": x[b], "gm": norm_mix[0], "w": np.ascontiguousarray(w0[:, cols]),
                     "cw": np.ascontiguousarray(conv_w[0][:, g * 128:(g + 1) * 128].T)})
    r0 = _run(_prog(("m0", S), lambda: build_m0(S)), maps)
    mixT0 = []
    for b in range(B):
        a = np.concatenate([np.asarray(r0[b * 4 + g]["mixo"])[0:128] for g in range(4)], axis=0)
        cc = np.concatenate([np.asarray(r0[b * 4 + g]["mixo"])[128:256] for g in range(4)], axis=0)
        mixT0.append(np.concatenate([a, cc], axis=0))

    maps = []
    for c in range(nco):
        b, r = c // 4, c % 4
        ts = slice(r * NT, (r + 1) * NT)
        maps.append({"mixT": np.ascontiguousarray(mixT0[b][:, ts]), "x": np.ascontiguousarray(x[b, ts]),
                     "wo": w_out_even[0], "g2": norm_mlp[0], "wu": w_up[0], "wdn": w_down[0], "g3": norm_mix[1]})
    r1 = _run(_prog(("d0", NT), lambda: build_dense(NT, False)), maps)
    hT1 = [np.ascontiguousarray(np.concatenate([np.asarray(r1[b * 4 + r]["hTo"]) for r in range(4)], axis=1))
           for b in range(B)]

    w1 = w_in_odd[0]
    maps = []
    for c in range(nco):
        b, g = c // 4, c % 4
        cols = np.concatenate([np.arange(sec * 1024 + g * 256, sec * 1024 + (g + 1) * 256) for sec in range(3)])
        maps.append({"hT": hT1[b], "w": np.ascontiguousarray(w1[:, cols]), "lamv": lamv,
                     "sg": np.ascontiguousarray(subln_g[0].reshape(128, 1))})
    r2 = _run(_prog(("m1", S), lambda: build_m1(S, lam_init)), maps)
    mixT1 = [np.concatenate([np.asarray(r2[b * 4 + g]["mixo"]) for g in range(4)], axis=0) for b in range(B)]

    maps = []
    for c in range(nco):
        b, r = c // 4, c % 4
        ts = slice(r * NT, (r + 1) * NT)
        maps.append({"mixT": np.ascontiguousarray(mixT1[b][:, ts]), "x": np.asarray(r1[c]["xo"]),
                     "wo": w_out_odd[0], "g2": norm_mlp[1], "wu": w_up[1], "wdn": w_down[1], "g3": norm_final})
    r3 = _run(_prog(("d1", NT), lambda: build_dense(NT, True)), maps)
    out = np.stack([np.concatenate([np.asarray(r3[b * 4 + r]["y"]) for r in range(4)], axis=0) for b in range(B)], axis=0)
    return out.astype(np.float32)
```
